# Optimizing a Trainium2 kernel written in Bass

```python
import math
import jax, jax.numpy as jnp
from jax import lax
import numpy as np

D_MODEL = 2048
BATCH = 16
SEQ = 2048
DEPTH = 4
DEC_BATCH = 8
DEC_SEQ = 32
PAST_LEN = 2048

CHUNK = 64
N_MIXERS = 2
HEAD_DIM = 128
N_HEADS_A = 16
N_KV_HEADS_A = 4
N_IDX_HEADS = 16
IDX_DIM = 64
TOPK_MAX = 256
N_HEADS_B = 16
DIFF_DIM = 64
D_FF = 5632
PLE_DIM = 256
ROPE_FRACTION = 4
ROPE_THETA = 500000.0
Q_BLOCK = 128
DSA_Q_BLOCK = 64
LN_EPS = 1e-5
DEEPNORM_ALPHA = (2 * DEPTH) ** 0.25
DEEPNORM_BETA = (8 * DEPTH) ** -0.25
N_A_LAYERS = (DEPTH + 1) // 2
N_B_LAYERS = DEPTH // 2
A_Q = N_HEADS_A * HEAD_DIM
A_KV = N_KV_HEADS_A * HEAD_DIM
A_QI = N_IDX_HEADS * IDX_DIM
A_SPLITS = [A_Q, A_Q + A_KV, A_Q + 2 * A_KV, A_Q + 2 * A_KV + A_QI, A_Q + 2 * A_KV + A_QI + IDX_DIM]
A_IN = A_Q + 2 * A_KV + A_QI + IDX_DIM + N_IDX_HEADS
B_QK = N_HEADS_B * 2 * DIFF_DIM
B_V = N_HEADS_B * 2 * DIFF_DIM
B_IN = 2 * B_QK + B_V

kernel_name = "hybrid_dsa_diffattn_streaming_encoder_step"


def chunk_mask(qpos, kpos):
    return (kpos // CHUNK)[None, :] <= (qpos // CHUNK)[:, None]


def layer_norm(x, g, b):
    xf = x.astype(jnp.float32)
    mu = jnp.mean(xf, axis=-1, keepdims=True)
    xc = xf - mu
    var = jnp.mean(xc * xc, axis=-1, keepdims=True)
    return (xc * lax.rsqrt(var + LN_EPS) * g + b).astype(x.dtype)


def post_norm(h, sub, g, b):
    return layer_norm(DEEPNORM_ALPHA * h + sub, g, b)


def swiglu(x, wg, wu, wd):
    return (jax.nn.silu(x @ wg) * (x @ wu)) @ wd


def partial_rope(x, pos):
    rd = x.shape[-1] // ROPE_FRACTION
    half = rd // 2
    inv_freq = ROPE_THETA ** (-jnp.arange(half, dtype=jnp.float32) / half)
    ang = pos.astype(jnp.float32)[:, None] * inv_freq[None, :]
    cos = jnp.cos(ang)[None, :, None, :].astype(x.dtype)
    sin = jnp.sin(ang)[None, :, None, :].astype(x.dtype)
    x1, x2, xp = x[..., :half], x[..., half:rd], x[..., rd:]
    return jnp.concatenate([x1 * cos - x2 * sin, x1 * sin + x2 * cos, xp], axis=-1)


def over_query_blocks(fn, block, qpos, *qs):
    nb = qpos.shape[0] // block
    qs_b = tuple(jnp.moveaxis(q.reshape(q.shape[0], nb, block, *q.shape[2:]), 1, 0) for q in qs)
    out = lax.map(lambda a: fn(a[0], *a[1]), (qpos.reshape(nb, block), qs_b))
    out = jnp.moveaxis(out, 0, 1)
    return out.reshape(out.shape[0], nb * block, *out.shape[3:])


def dsa_attend(qpos, q, qi, wi, kpos, k, v, ki, topk):
    B, Q = q.shape[0], q.shape[1]
    adm = chunk_mask(qpos, kpos)
    rel = jax.nn.relu(jnp.einsum('bqhd,bld->bqhl', qi, ki).astype(jnp.float32))
    score = jnp.einsum('bqhl,bqh->bql', rel, wi.astype(jnp.float32))
    score = jnp.where(adm[None], score, -jnp.inf)
    _, idx = lax.top_k(score, topk)
    ok = (kpos[idx] // CHUNK) <= (qpos // CHUNK)[None, :, None]
    kg = jax.vmap(lambda kb, ib: kb[ib])(k, idx)
    vg = jax.vmap(lambda vb, ib: vb[ib])(v, idx)
    qg = q.reshape(B, Q, N_KV_HEADS_A, N_HEADS_A // N_KV_HEADS_A, HEAD_DIM)
    s = jnp.einsum('bqgrd,bqkgd->bqgrk', qg, kg).astype(jnp.float32) * (HEAD_DIM ** -0.5)
    s = jnp.where(ok[:, :, None, None, :], s, -jnp.inf)
    p = jax.nn.softmax(s, axis=-1).astype(v.dtype)
    o = jnp.einsum('bqgrk,bqkgd->bqgrd', p, vg)
    return o.reshape(B, Q, A_Q)


def dsa_mixer(x, pos, w_in, w_out, past):
    B, L, _ = x.shape
    q, k, v, qi, ki, wi = jnp.split(x @ w_in, A_SPLITS, axis=-1)
    q = partial_rope(q.reshape(B, L, N_HEADS_A, HEAD_DIM), pos)
    k = partial_rope(k.reshape(B, L, N_KV_HEADS_A, HEAD_DIM), pos)
    v = v.reshape(B, L, N_KV_HEADS_A, HEAD_DIM)
    qi = partial_rope(qi.reshape(B, L, N_IDX_HEADS, IDX_DIM), pos)
    ki = partial_rope(ki[:, :, None, :], pos)[:, :, 0]
    wi = wi * ((N_IDX_HEADS * IDX_DIM) ** -0.5)
    if past is None:
        kpos, k_all, v_all, ki_all = pos, k, v, ki
    else:
        pk, pv, pki = past
        kpos = jnp.concatenate([jnp.arange(pk.shape[1], dtype=jnp.int32), pos])
        k_all = jnp.concatenate([pk, k], axis=1)
        v_all = jnp.concatenate([pv, v], axis=1)
        ki_all = jnp.concatenate([pki, ki], axis=1)
    topk = min(TOPK_MAX, k_all.shape[1] // 4)
    attend = lambda qp, qb, qib, wib: dsa_attend(qp, qb, qib, wib, kpos, k_all, v_all, ki_all, topk)
    if past is None:
        o = over_query_blocks(attend, DSA_Q_BLOCK, pos, q, qi, wi)
    else:
        o = attend(pos, q, qi, wi)
    return o @ w_out, (k, v, ki)


def diff_attend(qpos, q, kpos, k, v, lam, g, lam_init):
    B, Q = q.shape[0], q.shape[1]
    s = jnp.einsum('bqhcd,blhcd->bhcql', q, k).astype(jnp.float32) * (DIFF_DIM ** -0.5)
    s = jnp.where(chunk_mask(qpos, kpos)[None, None, None], s, -jnp.inf)
    p = jax.nn.softmax(s, axis=-1)
    a = (p[:, :, 0] - lam * p[:, :, 1]).astype(v.dtype)
    o = jnp.einsum('bhql,blhe->bqhe', a, v).astype(jnp.float32)
    o = o * lax.rsqrt(jnp.mean(o * o, axis=-1, keepdims=True) + LN_EPS) * g * (1.0 - lam_init)
    return o.reshape(B, Q, B_V).astype(v.dtype)


def diff_mixer(x, pos, w_in, w_out, lq1, lk1, lq2, lk2, g, lam_init, past):
    B, L, _ = x.shape
    q, k, v = jnp.split(x @ w_in, [B_QK, 2 * B_QK], axis=-1)
    q = partial_rope(q.reshape(B, L, 2 * N_HEADS_B, DIFF_DIM), pos).reshape(B, L, N_HEADS_B, 2, DIFF_DIM)
    k = partial_rope(k.reshape(B, L, 2 * N_HEADS_B, DIFF_DIM), pos).reshape(B, L, N_HEADS_B, 2, DIFF_DIM)
    v = v.reshape(B, L, N_HEADS_B, 2 * DIFF_DIM)
    lam = (jnp.exp(jnp.sum(lq1.astype(jnp.float32) * lk1.astype(jnp.float32)))
           - jnp.exp(jnp.sum(lq2.astype(jnp.float32) * lk2.astype(jnp.float32))) + lam_init)
    if past is None:
        kpos, k_all, v_all = pos, k, v
    else:
        pk, pv = past
        kpos = jnp.concatenate([jnp.arange(pk.shape[1], dtype=jnp.int32), pos])
        k_all = jnp.concatenate([pk, k], axis=1)
        v_all = jnp.concatenate([pv, v], axis=1)
    attend = lambda qp, qb: diff_attend(qp, qb, kpos, k_all, v_all, lam, g, lam_init)
    if past is None:
        o = over_query_blocks(attend, Q_BLOCK, pos, q)
    else:
        o = attend(pos, q)
    return o @ w_out, (k, v)


def run_trunk(x, p, pos, pasts, ln_g, ln_b, ffn_w_gate, ffn_w_up, ffn_w_down, ple_w_gate, ple_b_gate,
              ple_w_proj, a_w_in, a_w_out, b_w_in, b_w_out, b_lambda_q1, b_lambda_k1, b_lambda_q2,
              b_lambda_k2, b_subln):
    rows = []
    for i in range(DEPTH):
        x = post_norm(x, 0.5 * swiglu(x, ffn_w_gate[i, 0], ffn_w_up[i, 0], ffn_w_down[i, 0]), ln_g[i, 0], ln_b[i, 0])
        j = i // N_MIXERS
        if i % N_MIXERS == 0:
            mix, new = dsa_mixer(x, pos, a_w_in[j], a_w_out[j], pasts[i])
        else:
            lam_init = 0.8 - 0.6 * math.exp(-0.3 * i)
            mix, new = diff_mixer(x, pos, b_w_in[j], b_w_out[j], b_lambda_q1[j], b_lambda_k1[j],
                                  b_lambda_q2[j], b_lambda_k2[j], b_subln[j], lam_init, pasts[i])
        rows.append(new)
        x = post_norm(x, mix, ln_g[i, 1], ln_b[i, 1])
        x = post_norm(x, 0.5 * swiglu(x, ffn_w_gate[i, 1], ffn_w_up[i, 1], ffn_w_down[i, 1]), ln_g[i, 2], ln_b[i, 2])
        gate = jax.nn.sigmoid(x @ ple_w_gate[i] + ple_b_gate[i])
        x = post_norm(x, gate * (p[i] @ ple_w_proj[i]), ln_g[i, 3], ln_b[i, 3])
    return x, rows


def setup_inputs(seed: int = 0) -> dict:
    key = jax.random.key(seed)
    ks = iter(jax.random.split(key, 64))

    def nrm(shape, scale=1.0):
        return jax.random.normal(next(ks), shape, jnp.float32) * scale

    s_in = D_MODEL ** -0.5
    beta = DEEPNORM_BETA
    return {
        'x_prompt': nrm((BATCH, SEQ, D_MODEL)),
        'x_sample': nrm((DEC_BATCH, DEC_SEQ, D_MODEL)),
        'cache_l0_k': nrm((DEC_BATCH, PAST_LEN, N_KV_HEADS_A, HEAD_DIM)),
        'cache_l0_v': nrm((DEC_BATCH, PAST_LEN, N_KV_HEADS_A, HEAD_DIM), beta),
        'cache_l0_kidx': nrm((DEC_BATCH, PAST_LEN, IDX_DIM)),
        'cache_l1_k': nrm((DEC_BATCH, PAST_LEN, N_HEADS_B, 2, DIFF_DIM)),
        'cache_l1_v': nrm((DEC_BATCH, PAST_LEN, N_HEADS_B, 2 * DIFF_DIM), beta),
        'cache_l2_k': nrm((DEC_BATCH, PAST_LEN, N_KV_HEADS_A, HEAD_DIM)),
        'cache_l2_v': nrm((DEC_BATCH, PAST_LEN, N_KV_HEADS_A, HEAD_DIM), beta),
        'cache_l2_kidx': nrm((DEC_BATCH, PAST_LEN, IDX_DIM)),
        'cache_l3_k': nrm((DEC_BATCH, PAST_LEN, N_HEADS_B, 2, DIFF_DIM)),
        'cache_l3_v': nrm((DEC_BATCH, PAST_LEN, N_HEADS_B, 2 * DIFF_DIM), beta),
        'p_prompt': nrm((DEPTH, BATCH, SEQ, PLE_DIM)),
        'p_sample': nrm((DEPTH, DEC_BATCH, DEC_SEQ, PLE_DIM)),
        'ln_g': 1.0 + nrm((DEPTH, 4, D_MODEL), 0.01),
        'ln_b': nrm((DEPTH, 4, D_MODEL), 0.01),
        'ffn_w_gate': nrm((DEPTH, 2, D_MODEL, D_FF), s_in),
        'ffn_w_up': nrm((DEPTH, 2, D_MODEL, D_FF), s_in),
        'ffn_w_down': nrm((DEPTH, 2, D_FF, D_MODEL), D_FF ** -0.5 * beta),
        'ple_w_gate': nrm((DEPTH, D_MODEL, D_MODEL), s_in),
        'ple_b_gate': nrm((DEPTH, D_MODEL), 0.01),
        'ple_w_proj': nrm((DEPTH, PLE_DIM, D_MODEL), PLE_DIM ** -0.5 * beta),
        'a_w_in': jnp.concatenate([
            nrm((N_A_LAYERS, D_MODEL, A_Q), s_in),
            nrm((N_A_LAYERS, D_MODEL, A_KV), s_in),
            nrm((N_A_LAYERS, D_MODEL, A_KV), s_in * beta),
            nrm((N_A_LAYERS, D_MODEL, A_QI + IDX_DIM + N_IDX_HEADS), s_in)], axis=-1),
        'a_w_out': nrm((N_A_LAYERS, A_Q, D_MODEL), A_Q ** -0.5 * beta),
        'b_w_in': jnp.concatenate([
            nrm((N_B_LAYERS, D_MODEL, 2 * B_QK), s_in),
            nrm((N_B_LAYERS, D_MODEL, B_V), s_in * beta)], axis=-1),
        'b_w_out': nrm((N_B_LAYERS, B_V, D_MODEL), B_V ** -0.5 * beta),
        'b_lambda_q1': nrm((N_B_LAYERS, DIFF_DIM), 0.1),
        'b_lambda_k1': nrm((N_B_LAYERS, DIFF_DIM), 0.1),
        'b_lambda_q2': nrm((N_B_LAYERS, DIFF_DIM), 0.1),
        'b_lambda_k2': nrm((N_B_LAYERS, DIFF_DIM), 0.1),
        'b_subln': 1.0 + nrm((N_B_LAYERS, 2 * DIFF_DIM), 0.01),
    }


def reference(x_prompt, x_sample, cache_l0_k, cache_l0_v, cache_l0_kidx, cache_l1_k, cache_l1_v,
              cache_l2_k, cache_l2_v, cache_l2_kidx, cache_l3_k, cache_l3_v, p_prompt, p_sample,
              ln_g, ln_b, ffn_w_gate, ffn_w_up, ffn_w_down, ple_w_gate, ple_b_gate, ple_w_proj,
              a_w_in, a_w_out, b_w_in, b_w_out, b_lambda_q1, b_lambda_k1, b_lambda_q2, b_lambda_k2,
              b_subln):
    weights = (ln_g, ln_b, ffn_w_gate, ffn_w_up, ffn_w_down, ple_w_gate, ple_b_gate, ple_w_proj,
               a_w_in, a_w_out, b_w_in, b_w_out, b_lambda_q1, b_lambda_k1, b_lambda_q2, b_lambda_k2,
               b_subln)
    pos_p = jnp.arange(x_prompt.shape[1], dtype=jnp.int32)
    pos_s = cache_l0_k.shape[1] + jnp.arange(x_sample.shape[1], dtype=jnp.int32)
    pasts_s = [(cache_l0_k, cache_l0_v, cache_l0_kidx), (cache_l1_k, cache_l1_v),
               (cache_l2_k, cache_l2_v, cache_l2_kidx), (cache_l3_k, cache_l3_v)]
    y_prompt, rows_p = run_trunk(x_prompt, p_prompt, pos_p, [None] * DEPTH, *weights)
    y_sample, rows_s = run_trunk(x_sample, p_sample, pos_s, pasts_s, *weights)
    (l0_k_p, l0_v_p, l0_kidx_p), (l1_k_p, l1_v_p), (l2_k_p, l2_v_p, l2_kidx_p), (l3_k_p, l3_v_p) = rows_p
    (l0_k_s, l0_v_s, l0_kidx_s), (l1_k_s, l1_v_s), (l2_k_s, l2_v_s, l2_kidx_s), (l3_k_s, l3_v_s) = rows_s
    return (y_prompt, y_sample,
            l0_k_p, l0_v_p, l0_kidx_p, l0_k_s, l0_v_s, l0_kidx_s,
            l1_k_p, l1_v_p, l1_k_s, l1_v_s,
            l2_k_p, l2_v_p, l2_kidx_p, l2_k_s, l2_v_s, l2_kidx_s,
            l3_k_p, l3_v_p, l3_k_s, l3_v_s)
```

```python
import math
from contextlib import ExitStack
import numpy as np
import concourse.bass as bass
import concourse.mybir as mybir
from concourse.bass_utils import run_bass_kernel_spmd

F32 = mybir.dt.float32
BF16 = mybir.dt.bfloat16
AF = mybir.ActivationFunctionType
ALU = mybir.AluOpType
AX = mybir.AxisListType

D = 2048
DFF = 5632
PLE = 256
CHUNK = 64
TOPK = 256
LN_EPS = 1e-5
BIG = 30000.0
THETA = 500000.0
A_IN = 4176
B_IN = 6144


class Buf:
    __slots__ = ("name", "wr", "rd", "sem", "semval", "excl")

    def __init__(self, name, excl=False):
        self.name = name
        self.excl = excl
        self.wr = {}
        self.rd = {}
        self.sem = None
        self.semval = 0


class Sched:
    ENGS = ("pe", "act", "dve", "pool", "sp")

    def __init__(self, nc, stack):
        self.nc = nc
        self.stack = stack
        self.streams = {e: [] for e in self.ENGS}
        self.esem = {e: stack.enter_context(nc.semaphore("sem_" + e)) for e in ("pe", "act", "dve", "pool")}
        self.ecnt = {e: 0 for e in self.esem}
        self.seen = {e: {} for e in self.ENGS}
        self.dma_bufs = []
        self.nsem = 4

    def _wait(self, eng, ev):
        sem, val, own = ev
        if own == eng and eng == "pe":
            return
        k = id(sem)
        if self.seen[eng].get(k, 0) >= val:
            return
        self.seen[eng][k] = val
        self.streams[eng].append((0, sem, val))

    def _deps(self, eng, reads, writes):
        for b in reads:
            for ev in b.wr.values():
                self._wait(eng, ev)
            if b.excl:
                for k, ev in b.rd.items():
                    if k != eng:
                        self._wait(eng, ev)
        for b in writes:
            for ev in b.wr.values():
                self._wait(eng, ev)
            for ev in b.rd.values():
                self._wait(eng, ev)

    def op(self, eng, fn, reads=(), writes=(), signal=True):
        self._deps(eng, reads, writes)
        if signal:
            self.ecnt[eng] += 1
            ev = (self.esem[eng], self.ecnt[eng], eng)
            self.streams[eng].append((1, fn, self.esem[eng]))
            for b in reads:
                b.rd[eng] = ev
            for b in writes:
                b.wr[eng] = ev
                b.rd = {}
        else:
            self.streams[eng].append((1, fn, None))

    def dma(self, q, fn, reads, writes, owner):
        self._deps(q, reads, writes)
        qi = 1 if q == "pool" else 0
        if owner.sem is None:
            owner.sem = [None, None]
            owner.semval = [0, 0]
        if owner.sem[qi] is None:
            owner.sem[qi] = self.stack.enter_context(self.nc.semaphore("dsem%d_%s" % (qi, owner.name)))
            self.dma_bufs.append((owner, qi))
            self.nsem += 1
        owner.semval[qi] += 16
        sem = owner.sem[qi]
        ev = (sem, owner.semval[qi], None)
        k = id(sem)
        self.streams[q].append((2, fn, sem))
        for b in reads:
            b.rd[k] = ev
        for b in writes:
            b.wr[k] = ev
            b.rd = {}

    def fence(self, bufs):
        for b in bufs:
            for e in self.esem:
                if self.ecnt[e] > 0:
                    b.wr[e] = (self.esem[e], self.ecnt[e], e)

    def finish(self):
        for b, qi in self.dma_bufs:
            self._wait("sp", (b.sem[qi], b.semval[qi], None))
        for e in self.esem:
            if self.ecnt[e] > 0:
                self._wait("sp", (self.esem[e], self.ecnt[e], e))

    def replay(self, eng, h):
        for kind, a, b in self.streams[eng]:
            if kind == 0:
                h.wait_ge(a, b)
            elif kind == 1:
                ins = a(h)
                if b is not None:
                    ins.then_inc(b, 1)
            else:
                a(h).then_inc(b, 16)


def default_cfg():
    return dict(NPS=2, S=2048, DEPTH=4, PAST=2048, DEC=32, T=512, ALPHA=(2 * 4) ** 0.25, STAGES=None, DEBUG=False, MIXDBG=9)


def lam_init_of(i):
    return 0.8 - 0.6 * math.exp(-0.3 * i)


class Prog:
    def __init__(self, cfg):
        self.cfg = cfg
        self.NPS, self.S, self.DEPTH = cfg["NPS"], cfg["S"], cfg["DEPTH"]
        self.PAST, self.DEC, self.T = cfg["PAST"], cfg["DEC"], cfg["T"]
        self.ALPHA = cfg["ALPHA"]
        self.NA = (self.DEPTH + 1) // 2
        self.NB = self.DEPTH // 2
        self.NPOS = self.S + self.DEC
        self.LS = self.PAST + self.DEC
        self.LMAX = max(self.S, self.LS)
        self.KBMAX = (self.LMAX + 127) // 128

    def build(self):
        nc = bass.Bass("TRN2", target_bir_lowering=False)
        self.nc = nc
        with ExitStack() as stack:
            self.stack = stack
            self.sc = Sched(nc, stack)
            self.declare_dram()
            self.alloc_onchip()
            self.emit_program()
            self.sc.finish()
            with nc.Block() as block:
                @block.tensor
                def _(e):
                    self.sc.replay("pe", e)

                @block.scalar
                def _(e):
                    self.sc.replay("act", e)

                @block.vector
                def _(e):
                    self.sc.replay("dve", e)

                @block.gpsimd
                def _(e):
                    self.sc.replay("pool", e)

                @block.sync
                def _(e):
                    self.sc.replay("sp", e)
        return nc

    def din(self, name, shape):
        self.inputs[name] = tuple(shape)
        return self.nc.dram_tensor(name, list(shape), F32, kind="ExternalInput").ap()

    def dout(self, name, shape):
        self.outputs[name] = tuple(shape)
        return self.nc.dram_tensor(name, list(shape), F32, kind="ExternalOutput").ap()

    def dscr(self, name, shape, dt=BF16):
        return self.nc.dram_tensor(name, list(shape), dt).ap()

    def declare_dram(self):
        P = self
        self.inputs, self.outputs = {}, {}
        NT, DEP = P.NPS * P.S, P.DEPTH
        d = {}
        d["xp"] = self.din("xp", [NT, D])
        d["pp"] = self.din("pp", [DEP, NT, PLE])
        d["xs"] = self.din("xs", [P.DEC, D])
        d["ps"] = self.din("ps", [DEP, P.DEC, PLE])
        for l in range(DEP):
            if l % 2 == 0:
                d[f"ck{l}"] = self.din(f"ck{l}", [P.PAST, 512])
                d[f"cv{l}"] = self.din(f"cv{l}", [P.PAST, 512])
                d[f"ci{l}"] = self.din(f"ci{l}", [P.PAST, 64])
            else:
                d[f"ck{l}"] = self.din(f"ck{l}", [P.PAST, 2048])
                d[f"cv{l}"] = self.din(f"cv{l}", [P.PAST, 2048])
        d["ln_g"] = self.din("ln_g", [DEP * 4, D])
        d["ln_b"] = self.din("ln_b", [DEP * 4, D])
        d["wg"] = self.din("wg", [DEP * 2 * D, DFF])
        d["wu"] = self.din("wu", [DEP * 2 * D, DFF])
        d["wd"] = self.din("wd", [DEP * 2 * DFF, D])
        d["plg"] = self.din("plg", [DEP * D, D])
        d["plb"] = self.din("plb", [DEP, D])
        d["plp"] = self.din("plp", [DEP * PLE, D])
        d["awi"] = self.din("awi", [P.NA * D, A_IN])
        d["awo"] = self.din("awo", [P.NA * D, D])
        if P.NB:
            d["bwi"] = self.din("bwi", [P.NB * D, B_IN])
            d["bwo"] = self.din("bwo", [P.NB * D, D])
            d["blam"] = self.din("blam", [P.NB, 256])
            d["bsub"] = self.din("bsub", [P.NB, 128])
        d["ident"] = self.din("ident", [128, 128])
        d["rotA"] = self.din("rotA", [128, 128])
        d["rotB"] = self.din("rotB", [128, 128])
        d["tabfm"] = self.din("tabfm", [4, 128, P.NPOS])
        d["tabtm"] = self.din("tabtm", [P.NPOS, 48])
        d["yp"] = self.dout("yp", [NT, D])
        d["ys"] = self.dout("ys", [P.DEC, D])
        for l in range(DEP):
            w = 512 if l % 2 == 0 else 2048
            d[f"ok{l}p"] = self.dout(f"ok{l}p", [NT, w])
            d[f"ov{l}p"] = self.dout(f"ov{l}p", [NT, w])
            d[f"ok{l}s"] = self.dout(f"ok{l}s", [P.DEC, w])
            d[f"ov{l}s"] = self.dout(f"ov{l}s", [P.DEC, w])
            if l % 2 == 0:
                d[f"oi{l}p"] = self.dout(f"oi{l}p", [NT, 64])
                d[f"oi{l}s"] = self.dout(f"oi{l}s", [P.DEC, 64])
        if self.cfg["DEBUG"]:
            d["dbg"] = self.dout("dbg", [P.T, D])
        for nm in ("wg", "wu", "wd", "plg", "plb", "plp", "awi", "awo", "bwi", "bwo"):
            if nm in d:
                d[nm + "_b"] = self.dscr(nm + "_b", self.inputs[nm])
        for l in range(DEP):
            for q in range(P.NPS + 1):
                nh = 4 if l % 2 == 0 else 16
                d[f"kT{l}_{q}"] = self.dscr(f"kT{l}_{q}", [nh, 128, P.LMAX])
                d[f"vS{l}_{q}"] = self.dscr(f"vS{l}_{q}", [nh, 128, P.KBMAX, 128])
                if l % 2 == 0:
                    d[f"iT{l}_{q}"] = self.dscr(f"iT{l}_{q}", [128, P.LMAX])
        self.d = d
        self.dbuf = {k: Buf("D" + k) for k in d}

    def sb(self, name, shape, dt):
        t = self.stack.enter_context(self.nc.sbuf_tensor("sb_" + name, list(shape), dt))
        return t

    def alloc_onchip(self):
        nc, T = self.nc, self.T
        self.NS = T // 128
        NS = self.NS
        self.x = self.sb("x", [128, NS, D], F32)
        self.xB = [Buf(f"x{s}") for s in range(NS)]
        self.xT = self.sb("xT", [128, 16, T], BF16)
        self.xTB = [Buf(f"xT{s}") for s in range(NS)]
        self.NSLOT = 4
        self.ring = [self.sb(f"ring{i}", [128, 4096], BF16) for i in range(self.NSLOT)]
        self.ringB = [Buf(f"ring{i}") for i in range(self.NSLOT)]
        self.ring_i = 0
        self.gt = self.sb("gt", [128, D], F32)
        self.bt = self.sb("bt", [128, D], F32)
        self.gbB = Buf("gb")
        ARENA = 38912
        self.arena = self.sb("arena", [128, ARENA], BF16)
        a = self.arena
        self.hT = a[:, 0:44 * T].rearrange("p (c t) -> p c t", c=44)
        self.hTB = Buf("hT")
        self.ident_f = self.sb("ident_f", [128, 128], F32)
        self.ident = self.sb("ident", [128, 128], BF16)
        self.rotA = self.sb("rotA", [128, 128], BF16)
        self.rotB = self.sb("rotB", [128, 128], BF16)
        self.constB = Buf("const")
        self.xb = self.sb("xb", [128, D], BF16)
        self.xbB = Buf("xb")
        self.stats = self.sb("stats", [128, 4, 6], F32)
        self.mv = self.sb("mv", [128, 8], F32)
        self.stB = Buf("stats")
        self.sg = [self.sb(f"sg{i}", [128, T], F32) for i in range(2)]
        self.sgB = [Buf(f"sg{i}") for i in range(2)]
        LP = 2112
        o = 0
        self.qoT = a[:, o:o + 16 * T].rearrange("p (c t) -> p c t", c=16); o += 16 * T
        self.qiT = a[:, o:o + 8 * T].rearrange("p (c t) -> p c t", c=8); o += 8 * T
        self.kvk, self.kvv = [], []
        for i in range(2):
            self.kvk.append(a[:, o:o + LP]); o += LP
            self.kvv.append(a[:, o:o + 17 * 128].rearrange("p (k d) -> p k d", k=17)); o += 17 * 128
        self.kiT = a[:, o:o + LP]; o += LP
        self.A = a[:, o:o + 2 * LP].bitcast(F32); o += 2 * LP
        self.Bf = a[:, o:o + 2 * LP].bitcast(F32); o += 2 * LP
        self.Pm = a[:, o:o + LP]; o += LP
        self.PT = a[:, o:o + 17 * 128].rearrange("p (k t) -> p k t", k=17); o += 17 * 128
        self.otok = a[:, o:o + 2048]; o += 2048
        assert o <= ARENA, o
        self.qoTB = [Buf(f"qoT{s}") for s in range(NS)]
        self.qiTB = [Buf(f"qiT{s}") for s in range(NS)]
        self.kvB = [Buf("kv0"), Buf("kv1")]
        self.kiTB, self.AB, self.BfB, self.PmB, self.PTB, self.otokB = Buf("kiT"), Buf("A"), Buf("Bf"), Buf("Pm"), Buf("PT"), Buf("otok")
        self.attB = self.qoTB + self.qiTB + self.kvB + [self.kiTB, self.AB, self.BfB, self.PmB, self.PTB, self.otokB]
        self.ktok = self.sb("ktok", [128, 512], F32); self.ktokB = Buf("ktok")
        self.vtok = self.sb("vtok", [128, 512], F32); self.vtokB = Buf("vtok")
        self.kb16 = self.sb("kb16", [128, 512], BF16); self.kb16B = Buf("kb16")
        self.vb16 = self.sb("vb16", [128, 512], BF16); self.vb16B = Buf("vb16")
        self.kTs = self.sb("kTs", [128, 4, 128], BF16); self.kTsB = Buf("kTs")
        self.qraw = [self.sb(f"qraw{i}", [128, T], BF16) for i in range(2)]
        self.qrawB = [Buf(f"qraw{i}") for i in range(2)]
        self.rtmp = self.sb("rtmp", [128, 4, 64], F32); self.rtmpB = Buf("rtmp")
        self.tabf = self.sb("tabf", [128, 4, T], F32); self.tabfB = Buf("tabf")
        self.tabt = self.sb("tabt", [128, NS, 48], F32); self.tabtB = Buf("tabt")
        self.wi = self.sb("wi", [128, NS, 16], F32); self.wiB = Buf("wi")
        self.sm = self.sb("sm", [128, 64], F32); self.smB = Buf("sm")
        self.rs = self.sb("rs", [128, 16], F32); self.rsB = Buf("rs")
        self.m8 = self.sb("m8", [128, 8], F32)
        self.osm = self.sb("osm", [128, 2, 128], F32); self.osmB = Buf("osm")
        self.ptok = self.sb("ptok", [128, PLE], F32); self.ptokB = Buf("ptok")
        self.pb16 = self.sb("pb16", [128, PLE], BF16); self.pb16B = Buf("pb16")
        self.pT = self.sb("pT", [128, 2, T], BF16); self.pTB = Buf("pT")
        self.brow = self.xb; self.browB = self.xbB
        self.ones = self.sb("ones", [1, 128], BF16)
        self.cmask = None
        if self.NB:
            self.lamw = self.ptok
            self.lamt = self.sb("lamt", [128, 4 * self.NB], F32)
            self.gs = self.sb("gs", [128, self.NB, 128], F32)
        self.ps = self.stack.enter_context(nc.psum_tensor("psum_all", [128, 8, 512], F32))
        self.bank = [Buf(f"bank{i}", excl=True) for i in range(8)]
        self.cnt = 0
        self.cnt2 = self.cnt3 = self.cnt4 = self.cnt5 = 0

    def pbank(self, b):
        return self.ps[:, b, :]

    def pbank16(self, b):
        return self.ps[:, b, :].bitcast(BF16)

    def wblock(self, name, r0, nk, c0, ncols, wB):
        i = self.ring_i
        self.ring_i = (i + 1) % self.NSLOT
        src = self.d[name + "_b"][r0:r0 + nk * 128, c0:c0 + ncols].rearrange("(k p) c -> p k c", p=128)
        dst = self.ring[i][:, 0:nk * ncols].rearrange("p (k c) -> p k c", k=nk)
        self.sc.dma("sp", lambda e, dst=dst, src=src: e.dma_start(out=dst, in_=src),
                    reads=[wB], writes=[self.ringB[i]], owner=self.ringB[i])
        return self.ringB[i], dst

    def alt(self):
        self.cnt += 1
        return "act" if self.cnt % 2 else "dve"

    def copy(self, eng, out, in_, reads, writes):
        if eng == "act":
            self.sc.op("act", lambda e: e.activation(out=out, in_=in_, func=AF.Copy), reads, writes)
        elif eng == "dve":
            self.sc.op("dve", lambda e: e.tensor_copy(out=out, in_=in_), reads, writes)
        else:
            self.sc.op("pool", lambda e: e.tensor_copy(out=out, in_=in_), reads, writes)

    def prepass(self):
        sc, d = self.sc, self.d
        sc.dma("sp", lambda e: e.dma_start(out=self.ident_f[:], in_=d["ident"]), [], [self.constB], self.constB)
        self.copy("dve", self.ident[:], self.ident_f[:], [self.constB], [self.constB])
        for nm, t in (("rotA", self.rotA), ("rotB", self.rotB)):
            sc.dma("pool", lambda e, t=t, nm=nm: e.dma_start(out=t[:], in_=d[nm]), [], [self.constB], self.constB)
        sc.op("dve", lambda e: e.memset(self.ones[:], 1.0), [], [self.constB])
        for j in range(self.NB):
            li = lam_init_of(2 * j + 1)
            C = [self.constB]
            self.dmaq("sp", self.lamw[:], d["blam"][j:j + 1, :].partition_broadcast(128), [], C, self.constB)
            self.tt("dve", self.lamw[:, 0:64], self.lamw[:, 0:64], self.lamw[:, 64:128], ALU.mult, C, C)
            self.tt("dve", self.lamw[:, 128:192], self.lamw[:, 128:192], self.lamw[:, 192:256], ALU.mult, C, C)
            sc.op("dve", lambda e, j=j: e.reduce_sum(out=self.lamt[:, 4 * j + 1:4 * j + 2], in_=self.lamw[:, 0:64], axis=AX.X), C, C)
            sc.op("dve", lambda e, j=j: e.reduce_sum(out=self.lamt[:, 4 * j + 2:4 * j + 3], in_=self.lamw[:, 128:192], axis=AX.X), C, C)
            self.actf(self.lamt[:, 4 * j + 1:4 * j + 3], self.lamt[:, 4 * j + 1:4 * j + 3], AF.Exp, C, C)
            self.tt("dve", self.lamt[:, 4 * j + 3:4 * j + 4], self.lamt[:, 4 * j + 1:4 * j + 2], self.lamt[:, 4 * j + 2:4 * j + 3], ALU.subtract, C, C)
            self.ts("dve", self.lamt[:, 4 * j:4 * j + 1], self.lamt[:, 4 * j + 3:4 * j + 4], li, None, ALU.add, None, C, C)
            self.dmaq("sp", self.gs[:, j, :], d["bsub"][j:j + 1, :].partition_broadcast(128), [], C, self.constB)
            self.ts("dve", self.gs[:, j, :], self.gs[:, j, :], 1.0 - li, None, ALU.mult, None, C, C)
        order = []
        for l in range(self.DEPTH):
            j = l // 2
            order.append(("wg", (l * 2) * D, D)); order.append(("wu", (l * 2) * D, D)); order.append(("wd", (l * 2) * DFF, DFF))
            if l % 2 == 0:
                order.append(("awi", j * D, D)); order.append(("awo", j * D, D))
            else:
                order.append(("bwi", j * D, D)); order.append(("bwo", j * D, D))
            order.append(("wg", (l * 2 + 1) * D, D)); order.append(("wu", (l * 2 + 1) * D, D)); order.append(("wd", (l * 2 + 1) * DFF, DFF))
            order.append(("plg", l * D, D)); order.append(("plp", l * PLE, PLE))
        order.append(("plb", 0, self.DEPTH))
        self.wsl = {}
        for nm, r0, nr in order:
            b = Buf(f"W{nm}{r0}")
            self.wsl[(nm, r0)] = b
            nsp = 4 if nr >= 2048 else 1
            step = nr // nsp
            for i in range(nsp):
                a, bnd = r0 + i * step, r0 + (i + 1) * step
                sc.dma("pool", lambda e, nm=nm, a=a, bnd=bnd: e.dma_start(out=d[nm + "_b"][a:bnd, :], in_=d[nm][a:bnd, :]),
                       [], [b], b)

    def emit_program(self):
        self.prepass()
        P = self
        if P.DEC and (self.cfg["STAGES"] is None or "mix" in self.cfg["STAGES"]):
            for l in range(P.DEPTH):
                self.convert_cache(l)
        tiles = []
        for q in range(P.NPS):
            for ti in range(P.S // P.T):
                tiles.append((q, ti * P.T, P.T, False))
        if P.DEC:
            tiles.append((P.NPS, P.PAST, P.DEC, True))
        for (q, pos0, ntok, samp) in tiles:
            self.run_tile(q, pos0, ntok, samp)

    def subtiles(self, ntok):
        return [(s, min(128, ntok - s * 128)) for s in range((ntok + 127) // 128)]

    def run_tile(self, q, pos0, ntok, samp):
        P, sc, d = self, self.sc, self.d
        subs = self.subtiles(ntok)
        xin = d["xs"] if samp else d["xp"]
        row0 = 0 if samp else q * P.S + pos0
        for s, tn in subs:
            sc.dma("sp", lambda e, s=s, tn=tn: e.dma_start(out=self.x[:tn, s, :], in_=xin[row0 + s * 128: row0 + s * 128 + tn, :]),
                   [], [self.xB[s]], self.xB[s])
            self.make_xT(s, tn)
        stages = self.cfg["STAGES"]
        self.tile_tables(P.S if samp else pos0, subs, ntok)
        for l in range(P.DEPTH):
            if stages is None or "ffn" in stages:
                self.ffn(l, 0, subs, ntok)
            if stages is None or "mix" in stages:
                self.mixer(l, q, pos0, row0, subs, ntok, samp)
            if stages is None or "ffn2" in stages:
                self.ffn(l, 1, subs, ntok)
            if stages is None or "ple" in stages:
                self.ple(l, row0, subs, ntok, samp)
        yo = d["ys"] if samp else d["yp"]
        for s, tn in subs:
            sc.dma("pool", lambda e, s=s, tn=tn: e.dma_start(out=yo[row0 + s * 128: row0 + s * 128 + tn, :], in_=self.x[:tn, s, :]),
                   [self.xB[s]], [self.dbuf["ys" if samp else "yp"]], self.xB[s])

    def make_xT(self, s, tn):
        sc = self.sc
        self.copy("act", self.xb[:tn, :], self.x[:tn, s, :], [self.xB[s]], [self.xbB])
        for half in range(2):
            b = 6 + half
            pv = self.pbank16(b)
            for c in range(8):
                cc = half * 8 + c
                sc.op("pe", lambda e, c=c, cc=cc, pv=pv: e.transpose(out=pv[:, c * 128: c * 128 + tn], in_=self.xb[:tn, cc * 128:(cc + 1) * 128],
                                                                    identity=self.ident[:tn, :tn]),
                      reads=[self.xbB, self.constB], writes=[self.bank[b]], signal=(c == 7))
            src = pv[:, 0:1024].rearrange("p (c t) -> p c t", c=8)[:, :, 0:tn]
            dst = self.xT[:, half * 8:(half + 1) * 8, s * 128: s * 128 + tn]
            self.copy("dve" if half else "act", dst, src, [self.bank[b]], [self.xTB[s]])

    def load_gb(self, idx):
        sc, d = self.sc, self.d
        sc.dma("sp", lambda e: e.dma_start(out=self.gt[:], in_=d["ln_g"][idx:idx + 1, :].partition_broadcast(128)),
               [], [self.gbB], self.gbB)
        sc.dma("sp", lambda e: e.dma_start(out=self.bt[:], in_=d["ln_b"][idx:idx + 1, :].partition_broadcast(128)),
               [], [self.gbB], self.gbB)

    def post_norm(self, subs):
        sc = self.sc
        eps = LN_EPS / (self.ALPHA ** 2)
        for s, tn in subs:
            xs = self.x[:tn, s, :]
            for c in range(4):
                sc.op("dve", lambda e, c=c, xs=xs: e.bn_stats(out=self.stats[:tn, c, :], in_=xs[:, c * 512:(c + 1) * 512]),
                      [self.xB[s]], [self.stB])
            sc.op("dve", lambda e: e.bn_aggr(out=self.mv[:tn, 0:2], in_=self.stats[:tn, :, :]), [self.stB], [self.stB])
            sc.op("dve", lambda e: e.tensor_scalar(out=self.mv[:tn, 4:5], in0=self.mv[:tn, 1:2], scalar1=eps, scalar2=None,
                                                   op0=ALU.add), [self.stB], [self.stB])
            sc.op("act", lambda e: e.activation(out=self.mv[:tn, 5:6], in_=self.mv[:tn, 4:5], func=AF.Sqrt), [self.stB], [self.stB])
            sc.op("dve", lambda e: e.reciprocal(out=self.mv[:tn, 2:3], in_=self.mv[:tn, 5:6]), [self.stB], [self.stB])
            sc.op("dve", lambda e: e.scalar_tensor_tensor(out=self.mv[:tn, 3:4], in0=self.mv[:tn, 0:1], scalar=-1.0, in1=self.mv[:tn, 2:3],
                                                          op0=ALU.mult, op1=ALU.mult), [self.stB], [self.stB])
            sc.op("act", lambda e, xs=xs: e.activation(out=xs, in_=xs, func=AF.Identity, bias=self.mv[:tn, 3:4], scale=self.mv[:tn, 2:3]),
                  [self.stB, self.xB[s]], [self.xB[s]])
            sc.op("dve", lambda e, xs=xs: e.tensor_tensor(out=xs, in0=xs, in1=self.gt[:tn, :], op=ALU.mult), [self.gbB, self.xB[s]], [self.xB[s]])
            sc.op("pool", lambda e, xs=xs: e.tensor_tensor(out=xs, in0=xs, in1=self.bt[:tn, :], op=ALU.add), [self.gbB, self.xB[s]], [self.xB[s]])
            self.make_xT(s, tn)

    def ffn(self, l, j, subs, ntok):
        sc = self.sc
        self.load_gb(l * 4 + (0 if j == 0 else 2))
        wr0 = (l * 2 + j) * D
        wgB, wuB, wdB = self.wsl[("wg", wr0)], self.wsl[("wu", wr0)], self.wsl[("wd", (l * 2 + j) * DFF)]
        xTb = [self.xTB[s] for s, _ in subs]
        sc.fence([self.hTB])
        for fg in range(DFF // 256):
            gB, gap = self.wblock("wg", wr0, 16, fg * 256, 256, wgB)
            uB, uap = self.wblock("wu", wr0, 16, fg * 256, 256, wuB)
            for ci in range(2):
                f = fg * 2 + ci
                par = f % 2
                bg, bu = 2 * par, 2 * par + 1
                for (bk, wB, wap) in ((bg, gB, gap), (bu, uB, uap)):
                    for k in range(16):
                        sc.op("pe", lambda e, bk=bk, wap=wap, k=k, ci=ci: e.matmul(self.pbank(bk)[:, :ntok], lhsT=wap[:, k, ci * 128:(ci + 1) * 128],
                                                                                  rhs=self.xT[:, k, :ntok], start=(k == 0), stop=(k == 15)),
                              reads=[wB] + xTb, writes=[self.bank[bk]], signal=(k == 15))
                sgt, sgB = self.sg[par], self.sgB[par]
                sc.op("act", lambda e, bg=bg, sgt=sgt: e.activation(out=sgt[:, :ntok], in_=self.pbank(bg)[:, :ntok], func=AF.Silu),
                      [self.bank[bg]], [sgB])
                sc.op("dve", lambda e, bu=bu, sgt=sgt, f=f: e.tensor_tensor(out=self.hT[:, f, :ntok], in0=sgt[:, :ntok], in1=self.pbank(bu)[:, :ntok], op=ALU.mult),
                      [self.bank[bu], sgB], [self.hTB])
        cres = 0.5 / self.ALPHA
        for half in range(2):
            for fc4 in range(11):
                wB, wap = self.wblock("wd", (l * 2 + j) * DFF + fc4 * 512, 4, half * 1024, 1024, wdB)
                for fcl in range(4):
                    fc = fc4 * 4 + fcl
                    for s, tn in subs:
                        for jj in range(2):
                            bk = s * 2 + jj
                            sc.op("pe", lambda e, bk=bk, tn=tn, s=s, fc=fc, wap=wap, fcl=fcl, jj=jj: e.matmul(
                                self.pbank(bk)[:tn, :], lhsT=self.hT[:, fc, s * 128: s * 128 + tn], rhs=wap[:, fcl, jj * 512:(jj + 1) * 512],
                                start=(fc == 0), stop=(fc == 43)),
                                reads=[wB, self.hTB], writes=[self.bank[bk]],
                                signal=(fc == 43 or (fcl == 3 and s == subs[-1][0] and jj == 1)))
            for s, tn in subs:
                for jj in range(2):
                    bk = s * 2 + jj
                    xs = self.x[:tn, s, half * 1024 + jj * 512: half * 1024 + (jj + 1) * 512]
                    sc.op("dve", lambda e, bk=bk, tn=tn, xs=xs: e.scalar_tensor_tensor(out=xs, in0=self.pbank(bk)[:tn, :], scalar=cres, in1=xs,
                                                                                       op0=ALU.mult, op1=ALU.add),
                          [self.bank[bk], self.xB[s]], [self.xB[s]])
        self.post_norm(subs)


    def mm(self, out, lhsT, rhs, start, stop, reads, writes, signal):
        self.sc.op("pe", lambda e: e.matmul(out, lhsT=lhsT, rhs=rhs, start=start, stop=stop), reads, writes, signal)

    def tr(self, out, in_, n, reads, writes, signal):
        self.sc.op("pe", lambda e: e.transpose(out=out, in_=in_, identity=self.ident[:n, :n]), list(reads) + [self.constB], writes, signal)

    def actf(self, out, in_, func, reads, writes, bias=None, scale=None, accum=None):
        kw = {}
        if bias is not None:
            kw["bias"] = bias
        if scale is not None:
            kw["scale"] = scale
        if accum is not None:
            kw["accum_out"] = accum
        self.sc.op("act", lambda e: e.activation(out=out, in_=in_, func=func, **kw), reads, writes)

    def tt(self, eng, out, in0, in1, op, reads, writes):
        self.sc.op(eng, lambda e: e.tensor_tensor(out=out, in0=in0, in1=in1, op=op), reads, writes)

    def ts(self, eng, out, in0, s1, s2, op0, op1, reads, writes):
        if s2 is None:
            self.sc.op(eng, lambda e: e.tensor_scalar(out=out, in0=in0, scalar1=s1, scalar2=None, op0=op0), reads, writes)
        else:
            self.sc.op(eng, lambda e: e.tensor_scalar(out=out, in0=in0, scalar1=s1, scalar2=s2, op0=op0, op1=op1), reads, writes)

    def stt(self, eng, out, in0, scalar, in1, op0, op1, reads, writes):
        self.sc.op(eng, lambda e: e.scalar_tensor_tensor(out=out, in0=in0, scalar=scalar, in1=in1, op0=op0, op1=op1), reads, writes)

    def dmaq(self, q, out, in_, reads, writes, owner):
        self.sc.dma(q, lambda e: e.dma_start(out=out, in_=in_), reads, writes, owner)

    def tm_linear(self, wname, wB, r0, KC, c0, ncols, pw, src, srcB, subs, consume, b0=0, bias=None):
        nb = (pw + 511) // 512
        kcb = max(1, min(KC, 4096 // pw))
        for pc in range(0, ncols, pw):
            for kb in range(0, KC, kcb):
                nk = min(kcb, KC - kb)
                blkB, blk = self.wblock(wname, r0 + kb * 128, nk, c0 + pc, pw, wB)
                for kl in range(nk):
                    k = kb + kl
                    for si, (s, tn) in enumerate(subs):
                        for jj in range(nb):
                            w = min(512, pw - jj * 512)
                            bk = b0 + si * nb + jj
                            stop = (k == KC - 1) and bias is None
                            lastuse = (kl == nk - 1 and si == len(subs) - 1 and jj == nb - 1)
                            self.mm(self.pbank(bk)[:tn, :w], src[:, k, s * 128: s * 128 + tn], blk[:, kl, jj * 512: jj * 512 + w],
                                    k == 0, stop, [blkB] + list(srcB), [self.bank[bk]], stop or lastuse)
            if bias is not None:
                bap, bB = bias
                for si, (s, tn) in enumerate(subs):
                    for jj in range(nb):
                        w = min(512, pw - jj * 512)
                        bk = b0 + si * nb + jj
                        self.mm(self.pbank(bk)[:tn, :w], self.ones[0:1, :tn], bap[0:1, pc + jj * 512: pc + jj * 512 + w],
                                False, True, [bB, self.constB], [self.bank[bk]], True)
            for si, (s, tn) in enumerate(subs):
                for jj in range(nb):
                    w = min(512, pw - jj * 512)
                    consume(pc, s, tn, jj, w, b0 + si * nb + jj)

    def fm_linear(self, wname, wB, r0, c0, nchunks, src, srcB, ntok, consume, banks=(4, 5, 6, 7)):
        for bi in range((nchunks + 1) // 2):
            ncb = min(2, nchunks - bi * 2)
            blkB, blk = self.wblock(wname, r0, 16, c0 + bi * 256, ncb * 128, wB)
            for c2 in range(ncb):
                ci = bi * 2 + c2
                bk = banks[ci % len(banks)]
                for k in range(16):
                    self.mm(self.pbank(bk)[:, :ntok], blk[:, k, c2 * 128:(c2 + 1) * 128], src[:, k, :ntok], k == 0, k == 15,
                            [blkB] + list(srcB), [self.bank[bk]], k == 15)
                consume(ci, bk)

    def residual_consume(self, c):
        def f(pc, s, tn, jj, w, bk):
            xs = self.x[:tn, s, pc + jj * 512: pc + jj * 512 + w]
            self.stt("dve", xs, self.pbank(bk)[:tn, :w], c, xs, ALU.mult, ALU.add, [self.bank[bk], self.xB[s]], [self.xB[s]])
        return f

    def rope_fm(self, bk, kind, dst, dstBs, ntok):
        qi = self.cnt2 % 2
        self.cnt2 += 1
        qr, qrB = self.qraw[qi], self.qrawB[qi]
        rb = 2 + qi
        self.actf(qr[:, :ntok], self.pbank(bk)[:, :ntok], AF.Copy, [self.bank[bk]], [qrB])
        rot = self.rotA if kind == 0 else self.rotB
        self.mm(self.pbank(rb)[:, :ntok], rot[:, :], qr[:, :ntok], True, True, [qrB, self.constB], [self.bank[rb]], True)
        t1, t1B, t2, t2B = self.sg[0], self.sgB[0], self.sg[1], self.sgB[1]
        cos, sin = self.tabf[:, 2 * kind, :ntok], self.tabf[:, 2 * kind + 1, :ntok]
        self.tt("dve", t1[:, :ntok], self.pbank(bk)[:, :ntok], cos, ALU.mult, [self.bank[bk], self.tabfB], [t1B])
        self.tt("dve", t2[:, :ntok], self.pbank(rb)[:, :ntok], sin, ALU.mult, [self.bank[rb], self.tabfB], [t2B])
        self.tt("pool", dst, t1[:, :ntok], t2[:, :ntok], ALU.add, [t1B, t2B], dstBs)

    def rope_tm(self, buf, bufB, tn, s, kind, nh, hd):
        half = 16 if kind == 0 else 8
        o = 0 if kind == 0 else 32
        v = buf[:tn, 0:nh * hd].rearrange("p (h d) -> p h d", h=nh)
        x1, x2 = v[:, :, 0:half], v[:, :, half:2 * half]
        t = [self.rtmp[:tn, i, 0:nh * half].rearrange("p (h d) -> p h d", h=nh) for i in range(4)]
        R = [bufB, self.tabtB, self.rtmpB]
        for h in range(nh):
            cos = self.tabt[:tn, s, o:o + half]
            sin = self.tabt[:tn, s, o + half:o + 2 * half]
            self.tt("dve", t[0][:, h, :], x1[:, h, :], cos, ALU.mult, R, [self.rtmpB])
            self.tt("dve", t[1][:, h, :], x2[:, h, :], sin, ALU.mult, R, [self.rtmpB])
            self.tt("dve", t[2][:, h, :], x1[:, h, :], sin, ALU.mult, R, [self.rtmpB])
            self.tt("dve", t[3][:, h, :], x2[:, h, :], cos, ALU.mult, R, [self.rtmpB])
        self.tt("dve", x1, t[0], t[1], ALU.subtract, R, [bufB])
        self.tt("dve", x2, t[2], t[3], ALU.add, R, [bufB])

    def kT_from_tok(self, l, q, grp, tn, pos):
        self.actf(self.kb16[:tn, :], self.ktok[:tn, :], AF.Copy, [self.ktokB], [self.kb16B])
        pv = self.pbank16(7)
        for i in range(4):
            self.tr(pv[:, i * 128: i * 128 + tn], self.kb16[:tn, i * 128:(i + 1) * 128], tn, [self.kb16B], [self.bank[7]], i == 3)
        self.copy("dve", self.kTs[:, :, :tn], pv[:, 0:512].rearrange("p (h t) -> p h t", h=4)[:, :, :tn], [self.bank[7]], [self.kTsB])
        nm = f"kT{l}_{q}"
        dst = self.d[nm][grp * 4:(grp + 1) * 4, :, pos:pos + tn].rearrange("h p t -> p h t")
        self.dmaq("pool", dst, self.kTs[:, :, :tn], [self.kTsB], [self.dbuf[nm]], self.kTsB)

    def ki_tail(self, l, q, tn, pos):
        self.actf(self.kb16[:tn, 0:64], self.ktok[:tn, 0:64], AF.Copy, [self.ktokB], [self.kb16B])
        self.actf(self.kb16[:tn, 64:128], self.ktok[:tn, 0:64], AF.Copy, [self.ktokB], [self.kb16B])
        pv = self.pbank16(7)
        self.tr(pv[:, 0:tn], self.kb16[:tn, 0:128], tn, [self.kb16B], [self.bank[7]], True)
        self.copy("dve", self.kTs[:, 0, :tn], pv[:, 0:tn], [self.bank[7]], [self.kTsB])
        nm = f"iT{l}_{q}"
        self.dmaq("pool", self.d[nm][:, pos:pos + tn], self.kTs[:, 0, :tn], [self.kTsB], [self.dbuf[nm]], self.kTsB)

    def k_consume(self, l, q, pos0, row0, samp, kind, grp0):
        okn = f"ok{l}s" if samp else f"ok{l}p"
        def f(pc, s, tn, jj, w, bk):
            grp = grp0 + pc // 512
            self.actf(self.ktok[:tn, :], self.pbank(bk)[:tn, :], AF.Copy, [self.bank[bk]], [self.ktokB])
            self.rope_tm(self.ktok, self.ktokB, tn, s, kind, 4 if kind == 0 else 8, 128 if kind == 0 else 64)
            self.dmaq("pool", self.d[okn][row0 + s * 128: row0 + s * 128 + tn, grp * 512:(grp + 1) * 512], self.ktok[:tn, :],
                      [self.ktokB], [self.dbuf[okn]], self.ktokB)
            self.kT_from_tok(l, q, grp, tn, pos0 + s * 128)
        return f

    def v_consume(self, l, q, pos0, row0, samp, grp0):
        ovn = f"ov{l}s" if samp else f"ov{l}p"
        def f(pc, s, tn, jj, w, bk):
            grp = grp0 + pc // 512
            self.actf(self.vtok[:tn, :], self.pbank(bk)[:tn, :], AF.Copy, [self.bank[bk]], [self.vtokB])
            self.dmaq("pool", self.d[ovn][row0 + s * 128: row0 + s * 128 + tn, grp * 512:(grp + 1) * 512], self.vtok[:tn, :],
                      [self.vtokB], [self.dbuf[ovn]], self.vtokB)
            self.copy("dve", self.vb16[:tn, :], self.vtok[:tn, :], [self.vtokB], [self.vb16B])
            nm = f"vS{l}_{q}"
            kb = (pos0 + s * 128) // 128
            dst = self.d[nm][grp * 4:(grp + 1) * 4, 0:tn, kb, :].rearrange("h p d -> p h d")
            self.dmaq("pool", dst, self.vb16[:tn, :].rearrange("p (h d) -> p h d", h=4), [self.vb16B], [self.dbuf[nm]], self.vb16B)
        return f

    def ki_consume(self, l, q, pos0, row0, samp):
        oin = f"oi{l}s" if samp else f"oi{l}p"
        def f(pc, s, tn, jj, w, bk):
            self.actf(self.ktok[:tn, 0:80], self.pbank(bk)[:tn, 0:80], AF.Copy, [self.bank[bk]], [self.ktokB])
            self.rope_tm(self.ktok, self.ktokB, tn, s, 1, 1, 64)
            self.dmaq("pool", self.d[oin][row0 + s * 128: row0 + s * 128 + tn, :], self.ktok[:tn, 0:64], [self.ktokB], [self.dbuf[oin]], self.ktokB)
            self.ts("dve", self.wi[:tn, s, :], self.ktok[:tn, 64:80], 1.0 / 32.0, None, ALU.mult, None, [self.ktokB], [self.wiB])
            self.ki_tail(l, q, tn, pos0 + s * 128)
        return f

    def convert_cache(self, l):
        P, d = self, self.d
        q = P.NPS
        dsa = (l % 2 == 0)
        nh = 4 if dsa else 16
        KBP = P.PAST // 128
        cvB = Buf(f"cvt{l}")
        for h in range(nh):
            src = d[f"cv{l}"][0:P.PAST, h * 128:(h + 1) * 128].rearrange("(kb p) d -> p kb d", p=128)
            self.dmaq("pool", d[f"vS{l}_{q}"][h, :, 0:KBP, :], src, [], [self.dbuf[f"vS{l}_{q}"]], cvB)
        for kb in range(KBP):
            for grp in range(nh // 4):
                self.dmaq("sp", self.ktok[:, :], d[f"ck{l}"][kb * 128:(kb + 1) * 128, grp * 512:(grp + 1) * 512], [], [self.ktokB], self.ktokB)
                self.kT_from_tok(l, q, grp, 128, kb * 128)
            if dsa:
                self.dmaq("sp", self.ktok[:, 0:64], d[f"ci{l}"][kb * 128:(kb + 1) * 128, :], [], [self.ktokB], self.ktokB)
                self.ki_tail(l, q, 128, kb * 128)

    def stiles(self, L):
        return [(c0, min(512, L - c0)) for c0 in range(0, L, 512)]

    def kblocks(self, L):
        return [(kb, min(128, L - kb * 128)) for kb in range((L + 127) // 128)]

    def load_kv(self, l, q, h, L, slot):
        KB = (L + 127) // 128
        self.dmaq("sp", self.kvk[slot][:, 0:L], self.d[f"kT{l}_{q}"][h, :, 0:L], [self.dbuf[f"kT{l}_{q}"]], [self.kvB[slot]], self.kvB[slot])
        KF, rem = L // 128, L % 128
        self.dmaq("sp", self.kvv[slot][:, 0:KF, :], self.d[f"vS{l}_{q}"][h, :, 0:KF, :], [self.dbuf[f"vS{l}_{q}"]], [self.kvB[slot]], self.kvB[slot])
        if rem:
            self.dmaq("sp", self.kvv[slot][0:rem, KF, :], self.d[f"vS{l}_{q}"][h, 0:rem, KF, :], [self.dbuf[f"vS{l}_{q}"]], [self.kvB[slot]], self.kvB[slot])

    def pv_from_P(self, tq, L, slot, obank_ap):
        kbs = self.kblocks(L)
        groups, cur = [], []
        for kb, nk in kbs:
            if nk < 128:
                if cur:
                    groups.append(cur)
                groups.append([(kb, nk)])
                cur = []
            else:
                cur.append((kb, nk))
                if len(cur) == 8:
                    groups.append(cur)
                    cur = []
        if cur:
            groups.append(cur)
        for gi, grp in enumerate(groups):
            b = 5 + (self.cnt3 % 2)
            self.cnt3 += 1
            pv = self.pbank16(b)
            nk = grp[0][1]
            for i, (kb, _) in enumerate(grp):
                self.tr(pv[:nk, i * 128: i * 128 + tq], self.Pm[:tq, kb * 128: kb * 128 + nk], tq, [self.PmB], [self.bank[b]], i == len(grp) - 1)
            n = len(grp)
            srcv = pv[:nk, 0:n * 128].rearrange("p (c t) -> p c t", c=n)[:, :, :tq]
            self.copy(self.alt(), self.PT[:nk, grp[0][0]:grp[0][0] + n, :tq], srcv, [self.bank[b]], [self.PTB])
        for kb, nk in kbs:
            self.mm(obank_ap, self.PT[:nk, kb, :tq], self.kvv[slot][:nk, kb, :], kb == 0, kb == len(kbs) - 1,
                    [self.PTB, self.kvB[slot]], [self.bank[7]], kb == len(kbs) - 1)

    def make_oT(self, s, tq):
        for half in range(2):
            b = 5 + half
            pv = self.pbank16(b)
            for c in range(8):
                cc = half * 8 + c
                self.tr(pv[:, c * 128: c * 128 + tq], self.otok[:tq, cc * 128:(cc + 1) * 128], tq, [self.otokB], [self.bank[b]], c == 7)
            srcv = pv[:, 0:1024].rearrange("p (c t) -> p c t", c=8)[:, :, 0:tq]
            self.copy(self.alt(), self.qoT[:, half * 8:(half + 1) * 8, s * 128: s * 128 + tq], srcv, [self.bank[b]], [self.qoTB[s]])

    def attend_dsa(self, l, q, s, tq, L, masked):
        P = self
        TOPK = min(256, ((P.PAST + P.DEC) if not masked else P.S) // 4)
        assert TOPK % 8 == 0
        scale = 128 ** -0.5
        st = self.stiles(L)
        qc = slice(s * 128, s * 128 + tq)
        sbanks = [self.bank[i] for i in range(len(st))]
        self.dmaq("sp", self.kiT[:, 0:L], self.d[f"iT{l}_{q}"][:, 0:L], [self.dbuf[f"iT{l}_{q}"]], [self.kiTB], self.kiTB)
        for (c0, w) in st:
            for h in range(16):
                pb = 64 * (h % 2)
                i2 = self.cnt4 % 2
                self.cnt4 += 1
                bk = 5 + i2
                tmp, tmpB = self.sg[i2], self.sgB[i2]
                self.mm(self.pbank(bk)[:tq, :w], self.qiT[pb:pb + 64, h // 2, qc], self.kiT[pb:pb + 64, c0:c0 + w], True, True,
                        [self.qiTB[s], self.kiTB], [self.bank[bk]], True)
                self.actf(tmp[:tq, :w], self.pbank(bk)[:tq, :w], AF.Relu, [self.bank[bk]], [tmpB])
                if h == 0:
                    self.ts("dve", self.A[:tq, c0:c0 + w], tmp[:tq, :w], self.wi[:tq, s, 0:1], None, ALU.mult, None, [tmpB, self.wiB], [self.AB])
                else:
                    self.stt("dve", self.A[:tq, c0:c0 + w], tmp[:tq, :w], self.wi[:tq, s, h:h + 1], self.A[:tq, c0:c0 + w], ALU.mult, ALU.add,
                             [tmpB, self.wiB, self.AB], [self.AB])
        if masked:
            self.sc.op("dve", lambda e: e.memset(self.A[0:64, L - 64:L], -BIG), [self.AB], [self.AB])
        if L > TOPK:
            self.copy("act", self.Bf[:tq, 0:L], self.A[:tq, 0:L], [self.AB], [self.BfB])
            for r in range(TOPK // 8):
                self.sc.op("dve", lambda e: e.max(out=self.m8[:tq, :], in_=self.Bf[:tq, 0:L]), [self.BfB], [self.smB])
                self.sc.op("dve", lambda e: e.match_replace(out=self.Bf[:tq, 0:L], in_to_replace=self.m8[:tq, :], in_values=self.Bf[:tq, 0:L],
                                                             imm_value=-BIG), [self.smB, self.BfB], [self.BfB])
            self.sc.op("dve", lambda e: e.tensor_reduce(out=self.sm[:tq, 0:1], in_=self.m8[:tq, :], axis=AX.X, op=ALU.min), [self.smB], [self.smB])
            self.ts("dve", self.Bf[:tq, 0:L], self.A[:tq, 0:L], self.sm[:tq, 0:1], -BIG, ALU.is_lt, ALU.mult, [self.AB, self.smB], [self.BfB])
        else:
            self.ts("dve", self.Bf[:tq, 0:L], self.A[:tq, 0:L], -BIG / 2, -BIG, ALU.is_lt, ALU.mult, [self.AB], [self.BfB])
        for g in range(4):
            slot = self.cnt5 % 2
            self.cnt5 += 1
            self.load_kv(l, q, g, L, slot)
            for hh in range(4):
                h = g * 4 + hh
                for i, (c0, w) in enumerate(st):
                    self.mm(self.pbank(i)[:tq, :w], self.qoT[:, h, qc], self.kvk[slot][:, c0:c0 + w], True, True,
                            [self.qoTB[s], self.kvB[slot]], [self.bank[i]], True)
                for i, (c0, w) in enumerate(st):
                    self.tt("dve", self.A[:tq, c0:c0 + w], self.pbank(i)[:tq, :w], self.Bf[:tq, c0:c0 + w], ALU.add,
                            [self.bank[i], self.BfB], [self.AB])
                self.sc.op("dve", lambda e: e.reduce_max(out=self.sm[:tq, 1:2], in_=self.A[:tq, 0:L], axis=AX.X), [self.AB], [self.smB])
                self.ts("dve", self.sm[:tq, 2:3], self.sm[:tq, 1:2], -scale, None, ALU.mult, None, [self.smB], [self.smB])
                self.actf(self.Pm[:tq, 0:L], self.A[:tq, 0:L], AF.Exp, [self.AB, self.smB], [self.PmB, self.rsB],
                          bias=self.sm[:tq, 2:3], scale=scale, accum=self.rs[:tq, h:h + 1])
                self.pv_from_P(tq, L, slot, self.pbank(7)[:tq, 0:128])
                self.sc.op("dve", lambda e, h=h: e.reciprocal(out=self.sm[:tq, 3:4], in_=self.rs[:tq, h:h + 1]), [self.rsB], [self.smB])
                self.actf(self.otok[:tq, h * 128:(h + 1) * 128], self.pbank(7)[:tq, 0:128], AF.Copy, [self.bank[7], self.smB], [self.otokB],
                          scale=self.sm[:tq, 3:4])
        self.make_oT(s, tq)

    def attend_diff(self, l, q, s, tq, L, masked):
        j = l // 2
        scale = 64 ** -0.5
        st = self.stiles(L)
        qc = slice(s * 128, s * 128 + tq)
        sflat = self.ps[:, 0:5, :].rearrange("p b n -> p (b n)")
        sB = [self.bank[i] for i in range(len(st))]
        for h in range(16):
            slot = self.cnt5 % 2
            self.cnt5 += 1
            self.load_kv(l, q, h, L, slot)
            for c in range(2):
                pb = 64 * c
                for i, (c0, w) in enumerate(st):
                    self.mm(self.pbank(i)[:tq, :w], self.qoT[pb:pb + 64, h, qc], self.kvk[slot][pb:pb + 64, c0:c0 + w], True, True,
                            [self.qoTB[s], self.kvB[slot]], [self.bank[i]], True)
                if masked:
                    parts = [(0, 64, L - 64), (64, 128, L)]
                else:
                    parts = [(0, tq, L)]
                for (p0, p1, ll) in parts:
                    self.sc.op("dve", lambda e, p0=p0, p1=p1, ll=ll: e.reduce_max(out=self.sm[p0:p1, 1:2], in_=sflat[p0:p1, 0:ll], axis=AX.X), sB, [self.smB])
                    self.ts("dve", self.sm[p0:p1, 2:3], self.sm[p0:p1, 1:2], -scale, None, ALU.mult, None, [self.smB], [self.smB])
                    self.actf(self.Pm[p0:p1, 0:ll], sflat[p0:p1, 0:ll], AF.Exp, sB + [self.smB], [self.PmB, self.rsB],
                              bias=self.sm[p0:p1, 2:3], scale=scale, accum=self.rs[p0:p1, c:c + 1])
                if masked:
                    self.sc.op("dve", lambda e: e.memset(self.Pm[0:64, L - 64:L], 0.0), [self.PmB], [self.PmB])
                self.pv_from_P(tq, L, slot, self.pbank(7)[:tq, c * 128:(c + 1) * 128])
            R = [self.smB, self.rsB]
            self.sc.op("dve", lambda e: e.reciprocal(out=self.sm[:tq, 4:6], in_=self.rs[:tq, 0:2]), [self.rsB], [self.smB])
            self.ts("dve", self.sm[:tq, 6:7], self.sm[:tq, 5:6], self.lamt[:tq, 4 * j:4 * j + 1], None, ALU.mult, None, [self.smB, self.constB], [self.smB])
            self.ts("dve", self.osm[:tq, 0, :], self.pbank(7)[:tq, 128:256], self.sm[:tq, 6:7], None, ALU.mult, None, [self.bank[7], self.smB], [self.osmB])
            self.stt("dve", self.osm[:tq, 1, :], self.pbank(7)[:tq, 0:128], self.sm[:tq, 4:5], self.osm[:tq, 0, :], ALU.mult, ALU.subtract,
                     [self.bank[7], self.smB, self.osmB], [self.osmB])
            self.actf(self.osm[:tq, 0, :], self.osm[:tq, 1, :], AF.Square, [self.osmB], [self.osmB, self.smB], accum=self.sm[:tq, 7:8])
            self.ts("dve", self.sm[:tq, 8:9], self.sm[:tq, 7:8], 1.0 / 128.0, LN_EPS, ALU.mult, ALU.add, [self.smB], [self.smB])
            self.actf(self.sm[:tq, 9:10], self.sm[:tq, 8:9], AF.Sqrt, [self.smB], [self.smB])
            self.sc.op("dve", lambda e: e.reciprocal(out=self.sm[:tq, 10:11], in_=self.sm[:tq, 9:10]), [self.smB], [self.smB])
            self.stt("dve", self.otok[:tq, h * 128:(h + 1) * 128], self.osm[:tq, 1, :], self.sm[:tq, 10:11], self.gs[:tq, j, :], ALU.mult, ALU.mult,
                     [self.osmB, self.smB, self.constB], [self.otokB])
        self.make_oT(s, tq)

    def tile_tables(self, tix0, subs, ntok):
        d = self.d
        for t in range(4):
            self.dmaq("sp", self.tabf[:, t, :ntok], d["tabfm"][t, :, tix0:tix0 + ntok], [], [self.tabfB], self.tabfB)
        for s, tn in subs:
            self.dmaq("sp", self.tabt[:tn, s, :], d["tabtm"][tix0 + s * 128: tix0 + s * 128 + tn, :], [], [self.tabtB], self.tabtB)

    def mixer(self, l, q, pos0, row0, subs, ntok, samp):
        P = self
        j = l // 2
        dsa = (l % 2 == 0)
        self.sc.fence(self.attB)
        self.load_gb(l * 4 + 1)
        xTb = [self.xTB[s] for s, _ in subs]
        qoB = [self.qoTB[s] for s, _ in subs]
        qiB = [self.qiTB[s] for s, _ in subs]
        dbg = self.cfg["MIXDBG"]
        if dbg < 1:
            return
        if dsa:
            wi_, wo_ = self.wsl[("awi", j * D)], self.wsl[("awo", j * D)]
            self.fm_linear("awi", wi_, j * D, 0, 16, self.xT, xTb, ntok,
                           lambda ci, bk: self.rope_fm(bk, 0, self.qoT[:, ci, :ntok], qoB, ntok))
            self.fm_linear("awi", wi_, j * D, 3072, 8, self.xT, xTb, ntok,
                           lambda ci, bk: self.rope_fm(bk, 1, self.qiT[:, ci, :ntok], qiB, ntok))
            if dbg < 2:
                return
            self.tm_linear("awi", wi_, j * D, 16, 2048, 512, 512, self.xT, xTb, subs, self.k_consume(l, q, pos0, row0, samp, 0, 0))
            self.tm_linear("awi", wi_, j * D, 16, 2560, 512, 512, self.xT, xTb, subs, self.v_consume(l, q, pos0, row0, samp, 0))
            self.tm_linear("awi", wi_, j * D, 16, 4096, 80, 80, self.xT, xTb, subs, self.ki_consume(l, q, pos0, row0, samp))
            if dbg < 3:
                return
            for s, tn in subs:
                L = (P.PAST + P.DEC) if samp else pos0 + (s + 1) * 128
                self.attend_dsa(l, q, s, tn, L, not samp)
            if dbg < 4:
                return
            self.tm_linear("awo", wo_, j * D, 16, 0, 2048, 1024, self.qoT, qoB, subs, self.residual_consume(1.0 / self.ALPHA))
        else:
            wi_, wo_ = self.wsl[("bwi", j * D)], self.wsl[("bwo", j * D)]
            self.fm_linear("bwi", wi_, j * D, 0, 16, self.xT, xTb, ntok,
                           lambda ci, bk: self.rope_fm(bk, 1, self.qoT[:, ci, :ntok], qoB, ntok))
            self.tm_linear("bwi", wi_, j * D, 16, 2048, 2048, 512, self.xT, xTb, subs, self.k_consume(l, q, pos0, row0, samp, 1, 0))
            self.tm_linear("bwi", wi_, j * D, 16, 4096, 2048, 512, self.xT, xTb, subs, self.v_consume(l, q, pos0, row0, samp, 0))
            for s, tn in subs:
                L = (P.PAST + P.DEC) if samp else pos0 + (s + 1) * 128
                self.attend_diff(l, q, s, tn, L, not samp)
            self.tm_linear("bwo", wo_, j * D, 16, 0, 2048, 1024, self.qoT, qoB, subs, self.residual_consume(1.0 / self.ALPHA))
        self.post_norm(subs)

    def ple(self, l, row0, subs, ntok, samp):
        d = self.d
        self.load_gb(l * 4 + 3)
        pin = d["ps"] if samp else d["pp"]
        wgB_, wpB_, wbB_ = self.wsl[("plg", l * D)], self.wsl[("plp", l * PLE)], self.wsl[("plb", 0)]
        self.dmaq("sp", self.brow[0:1, :], d["plb_b"][l:l + 1, :], [wbB_], [self.browB], self.browB)
        for s, tn in subs:
            self.dmaq("sp", self.ptok[:tn, :], pin[l, row0 + s * 128: row0 + s * 128 + tn, :], [], [self.ptokB], self.ptokB)
            self.copy("act", self.pb16[:tn, :], self.ptok[:tn, :], [self.ptokB], [self.pb16B])
            pv = self.pbank16(7)
            for c in range(2):
                self.tr(pv[:, c * 128: c * 128 + tn], self.pb16[:tn, c * 128:(c + 1) * 128], tn, [self.pb16B], [self.bank[7]], c == 1)
            self.copy("dve", self.pT[:, :, s * 128: s * 128 + tn], pv[:, 0:256].rearrange("p (c t) -> p c t", c=2)[:, :, :tn], [self.bank[7]], [self.pTB])
        xTb = [self.xTB[s] for s, _ in subs]
        inv_a = 1.0 / self.ALPHA
        for p0 in range(0, len(subs), 2):
            pair = subs[p0:p0 + 2]
            for half in range(2):
                wpB, wp = self.wblock("plp", l * PLE, 2, 0, 2048, wpB_)
                for si, (s, tn) in enumerate(pair):
                    for jj in range(2):
                        bk = 4 + si * 2 + jj
                        cs = half * 1024 + jj * 512
                        for c in range(2):
                            self.mm(self.pbank(bk)[:tn, :], self.pT[:, c, s * 128: s * 128 + tn], wp[:, c, cs:cs + 512], c == 0, c == 1,
                                    [self.pTB, wpB], [self.bank[bk]], c == 1)

                def cons(pc, s, tn, jj, w, bk, half=half, pair=pair):
                    si = [x[0] for x in pair].index(s)
                    pbk = 4 + si * 2 + jj
                    i2 = self.cnt4 % 2
                    self.cnt4 += 1
                    tmp, tmpB = self.sg[i2], self.sgB[i2]
                    self.actf(tmp[:tn, :], self.pbank(bk)[:tn, :], AF.Sigmoid, [self.bank[bk]], [tmpB])
                    self.tt("dve", tmp[:tn, :], tmp[:tn, :], self.pbank(pbk)[:tn, :], ALU.mult, [tmpB, self.bank[pbk]], [tmpB])
                    xs = self.x[:tn, s, pc + jj * 512: pc + (jj + 1) * 512]
                    self.stt("dve", xs, tmp[:tn, :], inv_a, xs, ALU.mult, ALU.add, [tmpB, self.xB[s]], [self.xB[s]])
                self.tm_linear("plg", wgB_, l * D, 16, half * 1024, 1024, 1024, self.xT, xTb, pair,
                               lambda pc, s, tn, jj, w, bk, half=half, cons=cons: cons(pc + half * 1024, s, tn, jj, w, bk),
                               b0=0, bias=(self.brow[0:1, half * 1024:(half + 1) * 1024], self.browB))
        self.post_norm(subs)


def rope_tables(P):
    pos = np.concatenate([np.arange(P.S), P.PAST + np.arange(P.DEC)]).astype(np.float32)
    out = {}
    for nm, half in (("A", 16), ("B", 8)):
        inv = (np.float32(THETA) ** (-(np.arange(half, dtype=np.float32) / np.float32(half)))).astype(np.float32)
        ang = (pos[:, None] * inv[None, :]).astype(np.float32)
        out[nm] = (np.cos(ang).astype(np.float32), np.sin(ang).astype(np.float32))
    tabtm = np.concatenate([out["A"][0], out["A"][1], out["B"][0], out["B"][1]], axis=1).astype(np.float32)
    tabfm = np.zeros((4, 128, P.NPOS), np.float32)
    tabfm[0] = 1.0
    tabfm[2] = 1.0
    cA, sA = out["A"]
    tabfm[0, 0:16] = cA.T; tabfm[0, 16:32] = cA.T
    tabfm[1, 0:16] = sA.T; tabfm[1, 16:32] = sA.T
    cB, sB = out["B"]
    for base in (0, 64):
        tabfm[2, base:base + 8] = cB.T; tabfm[2, base + 8:base + 16] = cB.T
        tabfm[3, base:base + 8] = sB.T; tabfm[3, base + 8:base + 16] = sB.T
    rotA = np.zeros((128, 128), np.float32)
    for m in range(16):
        rotA[m + 16, m] = -1.0
        rotA[m, m + 16] = 1.0
    rotB = np.zeros((128, 128), np.float32)
    for base in (0, 64):
        for m in range(8):
            rotB[base + m + 8, base + m] = -1.0
            rotB[base + m, base + m + 8] = 1.0
    return tabfm, tabtm, rotA, rotB


_PROG_CACHE = {}


def get_prog(cfg):
    key = tuple(sorted((k, str(v)) for k, v in cfg.items()))
    if key not in _PROG_CACHE:
        P = Prog(cfg)
        P.nc_built = P.build()
        _PROG_CACHE[key] = P
    return _PROG_CACHE[key]


def make_in_maps(P, inp, ncores):
    f = lambda a: np.ascontiguousarray(np.asarray(a, dtype=np.float32))
    DEP = P.DEPTH
    tabfm, tabtm, rotA, rotB = rope_tables(P)
    shared = {
        "ln_g": f(inp["ln_g"]).reshape(DEP * 4, D), "ln_b": f(inp["ln_b"]).reshape(DEP * 4, D),
        "wg": f(inp["ffn_w_gate"]).reshape(DEP * 2 * D, DFF), "wu": f(inp["ffn_w_up"]).reshape(DEP * 2 * D, DFF),
        "wd": f(inp["ffn_w_down"]).reshape(DEP * 2 * DFF, D),
        "plg": f(inp["ple_w_gate"]).reshape(DEP * D, D), "plb": f(inp["ple_b_gate"]).reshape(DEP, D),
        "plp": f(inp["ple_w_proj"]).reshape(DEP * PLE, D),
        "awi": f(inp["a_w_in"]).reshape(P.NA * D, A_IN), "awo": f(inp["a_w_out"]).reshape(P.NA * D, D),
        "ident": np.eye(128, dtype=np.float32), "rotA": rotA, "rotB": rotB, "tabfm": tabfm, "tabtm": tabtm,
    }
    if P.NB:
        shared["bwi"] = f(inp["b_w_in"]).reshape(P.NB * D, B_IN)
        shared["bwo"] = f(inp["b_w_out"]).reshape(P.NB * D, D)
        shared["blam"] = np.ascontiguousarray(np.concatenate([f(inp["b_lambda_q1"]), f(inp["b_lambda_k1"]),
                                                              f(inp["b_lambda_q2"]), f(inp["b_lambda_k2"])], axis=1))
        shared["bsub"] = f(inp["b_subln"])
    xp, pp = f(inp["x_prompt"]), f(inp["p_prompt"])
    xs, ps = f(inp["x_sample"]), f(inp["p_sample"])
    maps = []
    for c in range(ncores):
        m = dict(shared)
        sl = slice(c * P.NPS, (c + 1) * P.NPS)
        m["xp"] = np.ascontiguousarray(xp[sl].reshape(P.NPS * P.S, D))
        m["pp"] = np.ascontiguousarray(pp[:, sl].reshape(DEP, P.NPS * P.S, PLE))
        m["xs"] = np.ascontiguousarray(xs[c])
        m["ps"] = np.ascontiguousarray(ps[:, c])
        for l in range(DEP):
            m[f"ck{l}"] = np.ascontiguousarray(f(inp[f"cache_l{l}_k"])[c].reshape(P.PAST, -1))
            m[f"cv{l}"] = np.ascontiguousarray(f(inp[f"cache_l{l}_v"])[c].reshape(P.PAST, -1))
            if l % 2 == 0:
                m[f"ci{l}"] = np.ascontiguousarray(f(inp[f"cache_l{l}_kidx"])[c])
        maps.append(m)
    return maps


def assemble(P, res, ncores):
    def cat(name, shape_tail, per):
        return np.stack([r[name] for r in res], 0).reshape((ncores * per,) + shape_tail)
    outs = []
    yp = np.stack([r["yp"].reshape(P.NPS, P.S, D) for r in res], 0).reshape(ncores * P.NPS, P.S, D)
    ys = np.stack([r["ys"] for r in res], 0)
    outs += [yp, ys]
    for l in range(P.DEPTH):
        if l % 2 == 0:
            kshape, vshape = (4, 128), (4, 128)
        else:
            kshape, vshape = (16, 2, 64), (16, 128)
        kp = np.stack([r[f"ok{l}p"] for r in res], 0).reshape((ncores * P.NPS, P.S) + kshape)
        vp = np.stack([r[f"ov{l}p"] for r in res], 0).reshape((ncores * P.NPS, P.S) + vshape)
        ks = np.stack([r[f"ok{l}s"] for r in res], 0).reshape((ncores, P.DEC) + kshape)
        vs = np.stack([r[f"ov{l}s"] for r in res], 0).reshape((ncores, P.DEC) + vshape)
        if l % 2 == 0:
            ip = np.stack([r[f"oi{l}p"] for r in res], 0).reshape(ncores * P.NPS, P.S, 64)
            isx = np.stack([r[f"oi{l}s"] for r in res], 0).reshape(ncores, P.DEC, 64)
            outs += [kp, vp, ip, ks, vs, isx]
        else:
            outs += [kp, vp, ks, vs]
    return tuple(np.ascontiguousarray(o, dtype=np.float32) for o in outs)


def kernel(**inputs):
    cfg = default_cfg()
    P = get_prog(cfg)
    maps = make_in_maps(P, inputs, 8)
    r = run_bass_kernel_spmd(P.nc_built, maps, core_ids=list(range(8)))
    return assemble(P, r.results, 8)
```

```python
import math
from contextlib import ExitStack
import numpy as np
import concourse.bass as bass
import concourse.mybir as mybir
from concourse.bass_utils import run_bass_kernel_spmd

F32 = mybir.dt.float32
BF16 = mybir.dt.bfloat16
AF = mybir.ActivationFunctionType
ALU = mybir.AluOpType
AX = mybir.AxisListType

D = 2048
DFF = 5632
PLE = 256
CHUNK = 64
TOPK = 256
LN_EPS = 1e-5
BIG = 30000.0
THETA = 500000.0
A_IN = 4176
B_IN = 6144


class Buf:
    __slots__ = ("name", "wr", "rd", "sem", "semval", "excl")

    def __init__(self, name, excl=False):
        self.name = name
        self.excl = excl
        self.wr = {}
        self.rd = {}
        self.sem = None
        self.semval = 0


class Sched:
    ENGS = ("pe", "act", "dve", "pool", "sp")

    def __init__(self, nc, stack):
        self.nc = nc
        self.stack = stack
        self.streams = {e: [] for e in self.ENGS}
        self.esem = {e: stack.enter_context(nc.semaphore("sem_" + e)) for e in ("pe", "act", "dve", "pool")}
        self.ecnt = {e: 0 for e in self.esem}
        self.seen = {e: {} for e in self.ENGS}
        self.dma_bufs = []
        self.nsem = 4

    def _wait(self, eng, ev):
        sem, val, own = ev
        if own == eng and eng == "pe":
            return
        k = id(sem)
        if self.seen[eng].get(k, 0) >= val:
            return
        self.seen[eng][k] = val
        self.streams[eng].append((0, sem, val))

    def _deps(self, eng, reads, writes):
        for b in reads:
            for ev in b.wr.values():
                self._wait(eng, ev)
            if b.excl:
                for k, ev in b.rd.items():
                    if k != eng:
                        self._wait(eng, ev)
        for b in writes:
            for ev in b.wr.values():
                self._wait(eng, ev)
            for ev in b.rd.values():
                self._wait(eng, ev)

    def op(self, eng, fn, reads=(), writes=(), signal=True):
        self._deps(eng, reads, writes)
        if signal:
            self.ecnt[eng] += 1
            ev = (self.esem[eng], self.ecnt[eng], eng)
            self.streams[eng].append((1, fn, self.esem[eng]))
            for b in reads:
                b.rd[eng] = ev
            for b in writes:
                b.wr[eng] = ev
                b.rd = {}
        else:
            self.streams[eng].append((1, fn, None))

    def dma(self, q, fn, reads, writes, owner):
        self._deps(q, reads, writes)
        qi = 1 if q == "pool" else 0
        if owner.sem is None:
            owner.sem = [None, None]
            owner.semval = [0, 0]
        if owner.sem[qi] is None:
            owner.sem[qi] = self.stack.enter_context(self.nc.semaphore("dsem%d_%s" % (qi, owner.name)))
            self.dma_bufs.append((owner, qi))
            self.nsem += 1
        owner.semval[qi] += 16
        sem = owner.sem[qi]
        ev = (sem, owner.semval[qi], None)
        k = id(sem)
        self.streams[q].append((2, fn, sem))
        for b in reads:
            b.rd[k] = ev
        for b in writes:
            b.wr[k] = ev
            b.rd = {}

    def fence(self, bufs):
        for b in bufs:
            for e in self.esem:
                if self.ecnt[e] > 0:
                    b.wr[e] = (self.esem[e], self.ecnt[e], e)

    def finish(self):
        for b, qi in self.dma_bufs:
            self._wait("sp", (b.sem[qi], b.semval[qi], None))
        for e in self.esem:
            if self.ecnt[e] > 0:
                self._wait("sp", (self.esem[e], self.ecnt[e], e))

    def replay(self, eng, h):
        for kind, a, b in self.streams[eng]:
            if kind == 0:
                h.wait_ge(a, b)
            elif kind == 1:
                ins = a(h)
                if b is not None:
                    ins.then_inc(b, 1)
            else:
                a(h).then_inc(b, 16)


def default_cfg():
    return dict(NPS=2, S=2048, DEPTH=4, PAST=2048, DEC=32, T=512, ALPHA=(2 * 4) ** 0.25, STAGES=None, DEBUG=False, MIXDBG=9)


def lam_init_of(i):
    return 0.8 - 0.6 * math.exp(-0.3 * i)


class Prog:
    def __init__(self, cfg):
        self.cfg = cfg
        self.NPS, self.S, self.DEPTH = cfg["NPS"], cfg["S"], cfg["DEPTH"]
        self.PAST, self.DEC, self.T = cfg["PAST"], cfg["DEC"], cfg["T"]
        self.ALPHA = cfg["ALPHA"]
        self.NA = (self.DEPTH + 1) // 2
        self.NB = self.DEPTH // 2
        self.NPOS = self.S + self.DEC
        self.LS = self.PAST + self.DEC
        self.LMAX = max(self.S, self.LS)
        self.KBMAX = (self.LMAX + 127) // 128

    def build(self):
        nc = bass.Bass("TRN2", target_bir_lowering=False)
        self.nc = nc
        with ExitStack() as stack:
            self.stack = stack
            self.sc = Sched(nc, stack)
            self.declare_dram()
            self.alloc_onchip()
            self.emit_program()
            self.sc.finish()
            with nc.Block() as block:
                @block.tensor
                def _(e):
                    self.sc.replay("pe", e)

                @block.scalar
                def _(e):
                    self.sc.replay("act", e)

                @block.vector
                def _(e):
                    self.sc.replay("dve", e)

                @block.gpsimd
                def _(e):
                    self.sc.replay("pool", e)

                @block.sync
                def _(e):
                    self.sc.replay("sp", e)
        return nc

    def din(self, name, shape):
        self.inputs[name] = tuple(shape)
        return self.nc.dram_tensor(name, list(shape), F32, kind="ExternalInput").ap()

    def dout(self, name, shape):
        self.outputs[name] = tuple(shape)
        return self.nc.dram_tensor(name, list(shape), F32, kind="ExternalOutput").ap()

    def dscr(self, name, shape, dt=BF16):
        return self.nc.dram_tensor(name, list(shape), dt).ap()

    def declare_dram(self):
        P = self
        self.inputs, self.outputs = {}, {}
        NT, DEP = P.NPS * P.S, P.DEPTH
        d = {}
        d["xp"] = self.din("xp", [NT, D])
        d["pp"] = self.din("pp", [DEP, NT, PLE])
        d["xs"] = self.din("xs", [P.DEC, D])
        d["ps"] = self.din("ps", [DEP, P.DEC, PLE])
        for l in range(DEP):
            if l % 2 == 0:
                d[f"ck{l}"] = self.din(f"ck{l}", [P.PAST, 512])
                d[f"cv{l}"] = self.din(f"cv{l}", [P.PAST, 512])
                d[f"ci{l}"] = self.din(f"ci{l}", [P.PAST, 64])
            else:
                d[f"ck{l}"] = self.din(f"ck{l}", [P.PAST, 2048])
                d[f"cv{l}"] = self.din(f"cv{l}", [P.PAST, 2048])
        d["ln_g"] = self.din("ln_g", [DEP * 4, D])
        d["ln_b"] = self.din("ln_b", [DEP * 4, D])
        d["wg"] = self.din("wg", [DEP * 2 * D, DFF])
        d["wu"] = self.din("wu", [DEP * 2 * D, DFF])
        d["wd"] = self.din("wd", [DEP * 2 * DFF, D])
        d["plg"] = self.din("plg", [DEP * D, D])
        d["plb"] = self.din("plb", [DEP, D])
        d["plp"] = self.din("plp", [DEP * PLE, D])
        d["awi"] = self.din("awi", [P.NA * D, A_IN])
        d["awo"] = self.din("awo", [P.NA * D, D])
        if P.NB:
            d["bwi"] = self.din("bwi", [P.NB * D, B_IN])
            d["bwo"] = self.din("bwo", [P.NB * D, D])
            d["blam"] = self.din("blam", [P.NB, 256])
            d["bsub"] = self.din("bsub", [P.NB, 128])
        d["ident"] = self.din("ident", [128, 128])
        d["rotA"] = self.din("rotA", [128, 128])
        d["rotB"] = self.din("rotB", [128, 128])
        d["tabfm"] = self.din("tabfm", [6, 128, P.NPOS])
        d["tabtm"] = self.din("tabtm", [P.NPOS, 48])
        d["yp"] = self.dout("yp", [NT, D])
        d["ys"] = self.dout("ys", [P.DEC, D])
        for l in range(DEP):
            w = 512 if l % 2 == 0 else 2048
            d[f"ok{l}p"] = self.dout(f"ok{l}p", [NT, w])
            d[f"ov{l}p"] = self.dout(f"ov{l}p", [NT, w])
            d[f"ok{l}s"] = self.dout(f"ok{l}s", [P.DEC, w])
            d[f"ov{l}s"] = self.dout(f"ov{l}s", [P.DEC, w])
            if l % 2 == 0:
                d[f"oi{l}p"] = self.dout(f"oi{l}p", [NT, 64])
                d[f"oi{l}s"] = self.dout(f"oi{l}s", [P.DEC, 64])
        if self.cfg["DEBUG"]:
            d["dbg"] = self.dout("dbg", [P.T, D])
        for nm in ("wg", "wu", "wd", "plg", "plb", "plp", "awi", "awo", "bwi", "bwo"):
            if nm in d:
                d[nm + "_b"] = self.dscr(nm + "_b", self.inputs[nm])
        for l in range(DEP):
            for q in range(P.NPS + 1):
                nh = 4 if l % 2 == 0 else 16
                d[f"kT{l}_{q}"] = self.dscr(f"kT{l}_{q}", [nh, 128, P.LMAX])
                d[f"vS{l}_{q}"] = self.dscr(f"vS{l}_{q}", [nh, 128, P.KBMAX, 128])
                if l % 2 == 0:
                    d[f"iT{l}_{q}"] = self.dscr(f"iT{l}_{q}", [128, P.LMAX])
        self.d = d
        self.dbuf = {k: Buf("D" + k) for k in d}

    def sb(self, name, shape, dt):
        t = self.stack.enter_context(self.nc.sbuf_tensor("sb_" + name, list(shape), dt))
        return t

    def alloc_onchip(self):
        nc, T = self.nc, self.T
        self.NS = T // 128
        NS = self.NS
        self.x = self.sb("x", [128, NS, D], F32)
        self.xB = [Buf(f"x{s}") for s in range(NS)]
        self.xT = self.sb("xT", [128, 16, T], BF16)
        self.xTB = [Buf(f"xT{s}") for s in range(NS)]
        self.NSLOT = 4
        self.ring = [self.sb(f"ring{i}", [128, 4096], BF16) for i in range(self.NSLOT)]
        self.ringB = [Buf(f"ring{i}") for i in range(self.NSLOT)]
        self.ring_i = 0
        self.gt = self.sb("gt", [128, D], F32)
        self.bt = self.sb("bt", [128, D], F32)
        self.gbB = Buf("gb")
        ARENA = 38912
        self.arena = self.sb("arena", [128, ARENA], BF16)
        a = self.arena
        self.hT = a[:, 0:44 * T].rearrange("p (c t) -> p c t", c=44)
        self.hTB = Buf("hT")
        self.ident_f = self.sb("ident_f", [128, 128], F32)
        self.ident = self.sb("ident", [128, 128], BF16)
        self.rotA = self.sb("rotA", [128, 128], BF16)
        self.rotB = self.sb("rotB", [128, 128], BF16)
        self.constB = Buf("const")
        self.xb = self.sb("xb", [128, D], BF16)
        self.xbB = Buf("xb")
        self.stats = self.sb("stats", [128, 4, 6], F32)
        self.mv = self.sb("mv", [128, 8], F32)
        self.stB = Buf("stats")
        self.sg = [self.sb(f"sg{i}", [128, T], F32) for i in range(2)]
        self.sgB = [Buf(f"sg{i}") for i in range(2)]
        LP = 2112
        o = 0
        self.qoT = a[:, o:o + 16 * T].rearrange("p (c t) -> p c t", c=16); o += 16 * T
        self.qiT = a[:, o:o + 8 * T].rearrange("p (c t) -> p c t", c=8); o += 8 * T
        self.kvk, self.kvv = [], []
        for i in range(2):
            self.kvk.append(a[:, o:o + LP]); o += LP
            self.kvv.append(a[:, o:o + 17 * 129].rearrange("p (k d) -> p k d", k=17)); o += 17 * 129 + 1
        self.kiT = a[:, o:o + LP]; o += LP
        self.A = a[:, o:o + 2 * LP].bitcast(F32); o += 2 * LP
        self.Bb = a[:, o:o + LP]
        self.Bf = a[:, o:o + 2 * LP].bitcast(F32); o += 2 * LP
        self.Pm = a[:, o:o + LP]; o += LP
        self.PT = a[:, o:o + 17 * 128].rearrange("p (k t) -> p k t", k=17); o += 17 * 128
        self.otok = a[:, o:o + 2048]; o += 2048
        assert o <= ARENA, o
        self.qoTB = [Buf(f"qoT{s}") for s in range(NS)]
        self.qiTB = [Buf(f"qiT{s}") for s in range(NS)]
        self.kvB = [Buf("kv0"), Buf("kv1")]
        self.kiTB, self.AB, self.BfB, self.PmB, self.PTB, self.otokB = Buf("kiT"), Buf("A"), Buf("Bf"), Buf("Pm"), Buf("PT"), Buf("otok")
        self.attB = self.qoTB + self.qiTB + self.kvB + [self.kiTB, self.AB, self.BfB, self.PmB, self.PTB, self.otokB]
        self.ktok = self.sb("ktok", [128, 512], F32); self.ktokB = Buf("ktok")
        self.vtok = self.sb("vtok", [128, 512], F32); self.vtokB = Buf("vtok")
        self.kb16 = self.sb("kb16", [128, 512], BF16); self.kb16B = Buf("kb16")
        self.vb16 = self.sb("vb16", [128, 512], BF16); self.vb16B = Buf("vb16")
        self.kTs = self.sb("kTs", [128, 4, 128], BF16); self.kTsB = Buf("kTs")
        self.qraw = [self.sb(f"qraw{i}", [128, T], BF16) for i in range(2)]
        self.qrawB = [Buf(f"qraw{i}") for i in range(2)]
        self.rtmp = self.sb("rtmp", [128, 4, 64], F32); self.rtmpB = Buf("rtmp")
        self.tabf = self.sb("tabf", [128, 4, T], F32); self.tabfB = Buf("tabf")
        self.tabt = self.sb("tabt", [128, NS, 48], F32); self.tabtB = Buf("tabt")
        self.wi = self.sb("wi", [128, NS, 16], F32); self.wiB = Buf("wi")
        self.sm = self.sb("sm", [128, 64], F32); self.smB = Buf("sm")
        self.nmx = self.sb("nmx", [128, 2], F32); self.nmxB = Buf("nmx")
        self.nmx4 = self.sb("nmx4", [128, 8], F32); self.nmx4B = Buf("nmx4")
        self.rs = self.sb("rs", [128, 16], F32); self.rsB = Buf("rs")
        self.m8 = self.sb("m8", [128, 8], F32)
        self.osm = self.sb("osm", [128, 2, 128], F32); self.osmB = Buf("osm")
        self.ptok = self.sb("ptok", [128, PLE], F32); self.ptokB = Buf("ptok")
        self.pb16 = self.sb("pb16", [128, PLE], BF16); self.pb16B = Buf("pb16")
        self.pT = self.sb("pT", [128, 2, T], BF16); self.pTB = Buf("pT")
        self.brow = self.xb; self.browB = self.xbB
        self.ones = self.sb("ones", [1, 128], BF16)
        self.cmaskt = self.sb("cmask", [128, 64], BF16)
        self.cmask = None
        if self.NB:
            self.lamw = self.ptok
            self.lamt = self.sb("lamt", [128, 4 * self.NB], F32)
            self.gs = self.sb("gs", [128, self.NB, 128], F32)
        self.ps = self.stack.enter_context(nc.psum_tensor("psum_all", [128, 8, 512], F32))
        self.bank = [Buf(f"bank{i}", excl=True) for i in range(8)]
        self.cnt = 0
        self.cnt2 = self.cnt3 = self.cnt4 = self.cnt5 = 0

    def pbank(self, b):
        return self.ps[:, b, :]

    def pbank16(self, b):
        return self.ps[:, b, :].bitcast(BF16)

    def wblock(self, name, r0, nk, c0, ncols, wB):
        i = self.ring_i
        self.ring_i = (i + 1) % self.NSLOT
        src = self.d[name + "_b"][r0:r0 + nk * 128, c0:c0 + ncols].rearrange("(k p) c -> p k c", p=128)
        dst = self.ring[i][:, 0:nk * ncols].rearrange("p (k c) -> p k c", k=nk)
        self.sc.dma("sp", lambda e, dst=dst, src=src: e.dma_start(out=dst, in_=src),
                    reads=[wB], writes=[self.ringB[i]], owner=self.ringB[i])
        return self.ringB[i], dst

    def alt(self):
        self.cnt += 1
        return "act" if self.cnt % 2 else "dve"

    def copy(self, eng, out, in_, reads, writes):
        if eng == "act":
            self.sc.op("act", lambda e: e.activation(out=out, in_=in_, func=AF.Copy), reads, writes)
        elif eng == "dve":
            self.sc.op("dve", lambda e: e.tensor_copy(out=out, in_=in_), reads, writes)
        else:
            self.sc.op("pool", lambda e: e.tensor_copy(out=out, in_=in_), reads, writes)

    def prepass(self):
        sc, d = self.sc, self.d
        sc.dma("sp", lambda e: e.dma_start(out=self.ident_f[:], in_=d["ident"]), [], [self.constB], self.constB)
        self.copy("dve", self.ident[:], self.ident_f[:], [self.constB], [self.constB])
        for nm, t in (("rotA", self.rotA), ("rotB", self.rotB)):
            sc.dma("pool", lambda e, t=t, nm=nm: e.dma_start(out=t[:], in_=d[nm]), [], [self.constB], self.constB)
        sc.op("dve", lambda e: e.memset(self.ones[:], 1.0), [], [self.constB])
        sc.op("dve", lambda e: e.memset(self.cmaskt[0:64, :], -BIG), [], [self.constB])
        sc.op("dve", lambda e: e.memset(self.cmaskt[64:128, :], 0.0), [], [self.constB])
        for j in range(self.NB):
            li = lam_init_of(2 * j + 1)
            C = [self.constB]
            self.dmaq("sp", self.lamw[:], d["blam"][j:j + 1, :].partition_broadcast(128), [], C, self.constB)
            self.tt("dve", self.lamw[:, 0:64], self.lamw[:, 0:64], self.lamw[:, 64:128], ALU.mult, C, C)
            self.tt("dve", self.lamw[:, 128:192], self.lamw[:, 128:192], self.lamw[:, 192:256], ALU.mult, C, C)
            sc.op("dve", lambda e, j=j: e.reduce_sum(out=self.lamt[:, 4 * j + 1:4 * j + 2], in_=self.lamw[:, 0:64], axis=AX.X), C, C)
            sc.op("dve", lambda e, j=j: e.reduce_sum(out=self.lamt[:, 4 * j + 2:4 * j + 3], in_=self.lamw[:, 128:192], axis=AX.X), C, C)
            self.actf(self.lamt[:, 4 * j + 1:4 * j + 3], self.lamt[:, 4 * j + 1:4 * j + 3], AF.Exp, C, C)
            self.tt("dve", self.lamt[:, 4 * j + 3:4 * j + 4], self.lamt[:, 4 * j + 1:4 * j + 2], self.lamt[:, 4 * j + 2:4 * j + 3], ALU.subtract, C, C)
            self.ts("dve", self.lamt[:, 4 * j:4 * j + 1], self.lamt[:, 4 * j + 3:4 * j + 4], li, None, ALU.add, None, C, C)
            self.dmaq("sp", self.gs[:, j, :], d["bsub"][j:j + 1, :].partition_broadcast(128), [], C, self.constB)
            self.ts("dve", self.gs[:, j, :], self.gs[:, j, :], 1.0 - li, None, ALU.mult, None, C, C)
        order = []
        for l in range(self.DEPTH):
            j = l // 2
            order.append(("wg", (l * 2) * D, D)); order.append(("wu", (l * 2) * D, D)); order.append(("wd", (l * 2) * DFF, DFF))
            if l % 2 == 0:
                order.append(("awi", j * D, D)); order.append(("awo", j * D, D))
            else:
                order.append(("bwi", j * D, D)); order.append(("bwo", j * D, D))
            order.append(("wg", (l * 2 + 1) * D, D)); order.append(("wu", (l * 2 + 1) * D, D)); order.append(("wd", (l * 2 + 1) * DFF, DFF))
            order.append(("plg", l * D, D)); order.append(("plp", l * PLE, PLE))
        order.append(("plb", 0, self.DEPTH))
        self.wsl = {}
        for nm, r0, nr in order:
            b = Buf(f"W{nm}{r0}")
            self.wsl[(nm, r0)] = b
            nsp = 4 if nr >= 2048 else 1
            step = nr // nsp
            for i in range(nsp):
                a, bnd = r0 + i * step, r0 + (i + 1) * step
                sc.dma("pool", lambda e, nm=nm, a=a, bnd=bnd: e.dma_start(out=d[nm + "_b"][a:bnd, :], in_=d[nm][a:bnd, :]),
                       [], [b], b)

    def emit_program(self):
        self.prepass()
        P = self
        self.vcast, self.vcast_quota = [], 2
        self.do_conv = bool(P.DEC and (self.cfg["STAGES"] is None or "mix" in self.cfg["STAGES"]))
        tiles = []
        for q in range(P.NPS):
            for ti in range(P.S // P.T):
                tiles.append((q, ti * P.T, P.T, False))
        if P.DEC:
            tiles.append((P.NPS, P.PAST, P.DEC, True))
        if self.do_conv:
            self.conv_k = []
        for ti, (q, pos0, ntok, samp) in enumerate(tiles):
            if self.do_conv and ti == 0:
                for l in range(P.DEPTH):
                    self.convert_cache(l)
            if samp:
                while self.vcast:
                    self.vcast.pop(0)()
            self.run_tile(q, pos0, ntok, samp)

    def subtiles(self, ntok):
        return [(s, min(128, ntok - s * 128)) for s in range((ntok + 127) // 128)]

    def run_tile(self, q, pos0, ntok, samp):
        P, sc, d = self, self.sc, self.d
        subs = self.subtiles(ntok)
        xin = d["xs"] if samp else d["xp"]
        row0 = 0 if samp else q * P.S + pos0
        for s, tn in subs:
            sc.dma("sp", lambda e, s=s, tn=tn: e.dma_start(out=self.x[:tn, s, :], in_=xin[row0 + s * 128: row0 + s * 128 + tn, :]),
                   [], [self.xB[s]], self.xB[s])
            self.make_xT(s, tn)
        stages = self.cfg["STAGES"]
        self.tile_tables(P.S if samp else pos0, subs, ntok)
        for l in range(P.DEPTH):
            if stages is None or "ffn" in stages:
                self.ffn(l, 0, subs, ntok)
            if stages is None or "mix" in stages:
                self.mixer(l, q, pos0, row0, subs, ntok, samp)
            if stages is None or "ffn2" in stages:
                self.ffn(l, 1, subs, ntok)
            if stages is None or "ple" in stages:
                self.ple(l, row0, subs, ntok, samp)
        yo = d["ys"] if samp else d["yp"]
        for s, tn in subs:
            sc.dma("pool", lambda e, s=s, tn=tn: e.dma_start(out=yo[row0 + s * 128: row0 + s * 128 + tn, :], in_=self.x[:tn, s, :]),
                   [self.xB[s]], [self.dbuf["ys" if samp else "yp"]], self.xB[s])

    def make_xT(self, s, tn, evac=None):
        sc = self.sc
        self.copy("act", self.xb[:tn, :], self.x[:tn, s, :], [self.xB[s]], [self.xbB])
        for half in range(2):
            b = 6 + half
            pv = self.pbank16(b)
            for c in range(8):
                cc = half * 8 + c
                sc.op("pe", lambda e, c=c, cc=cc, pv=pv: e.transpose(out=pv[:, c * 128: c * 128 + tn], in_=self.xb[:tn, cc * 128:(cc + 1) * 128],
                                                                    identity=self.ident[:tn, :tn]),
                      reads=[self.xbB, self.constB], writes=[self.bank[b]], signal=(c == 7))
            src = pv[:, 0:1024].rearrange("p (c t) -> p c t", c=8)[:, :, 0:tn]
            dst = self.xT[:, half * 8:(half + 1) * 8, s * 128: s * 128 + tn]
            self.copy(evac or ("dve" if half else "act"), dst, src, [self.bank[b]], [self.xTB[s]])

    def load_gb(self, idx):
        sc, d = self.sc, self.d
        sc.dma("sp", lambda e: e.dma_start(out=self.gt[:], in_=d["ln_g"][idx:idx + 1, :].partition_broadcast(128)),
               [], [self.gbB], self.gbB)
        sc.dma("sp", lambda e: e.dma_start(out=self.bt[:], in_=d["ln_b"][idx:idx + 1, :].partition_broadcast(128)),
               [], [self.gbB], self.gbB)

    def post_norm(self, subs):
        sc = self.sc
        eps = LN_EPS / (self.ALPHA ** 2)
        for s, tn in subs:
            xs = self.x[:tn, s, :]
            for c in range(4):
                sc.op("dve", lambda e, c=c, xs=xs: e.bn_stats(out=self.stats[:tn, c, :], in_=xs[:, c * 512:(c + 1) * 512]),
                      [self.xB[s]], [self.stB])
            sc.op("dve", lambda e: e.bn_aggr(out=self.mv[:tn, 0:2], in_=self.stats[:tn, :, :]), [self.stB], [self.stB])
            sc.op("dve", lambda e: e.tensor_scalar(out=self.mv[:tn, 4:5], in0=self.mv[:tn, 1:2], scalar1=eps, scalar2=None,
                                                   op0=ALU.add), [self.stB], [self.stB])
            sc.op("act", lambda e: e.activation(out=self.mv[:tn, 5:6], in_=self.mv[:tn, 4:5], func=AF.Sqrt), [self.stB], [self.stB])
            sc.op("dve", lambda e: e.reciprocal(out=self.mv[:tn, 2:3], in_=self.mv[:tn, 5:6]), [self.stB], [self.stB])
            self.stt("dve", xs, xs, self.mv[:tn, 0:1], self.gt[:tn, :], ALU.subtract, ALU.mult, [self.stB, self.gbB, self.xB[s]], [self.xB[s]])
            self.stt("dve", xs, xs, self.mv[:tn, 2:3], self.bt[:tn, :], ALU.mult, ALU.add, [self.stB, self.gbB, self.xB[s]], [self.xB[s]])
        for s, tn in subs:
            self.make_xT(s, tn, evac="act")

    def ffn(self, l, j, subs, ntok):
        sc = self.sc
        self.load_gb(l * 4 + (0 if j == 0 else 2))
        wr0 = (l * 2 + j) * D
        wgB, wuB, wdB = self.wsl[("wg", wr0)], self.wsl[("wu", wr0)], self.wsl[("wd", (l * 2 + j) * DFF)]
        xTb = [self.xTB[s] for s, _ in subs]
        sc.fence([self.hTB])
        for fg in range(DFF // 256):
            gB, gap = self.wblock("wg", wr0, 16, fg * 256, 256, wgB)
            uB, uap = self.wblock("wu", wr0, 16, fg * 256, 256, wuB)
            for ci in range(2):
                f = fg * 2 + ci
                par = f % 2
                bg, bu = 2 * par, 2 * par + 1
                for (bk, wB, wap) in ((bg, gB, gap), (bu, uB, uap)):
                    for k in range(16):
                        sc.op("pe", lambda e, bk=bk, wap=wap, k=k, ci=ci: e.matmul(self.pbank(bk)[:, :ntok], lhsT=wap[:, k, ci * 128:(ci + 1) * 128],
                                                                                  rhs=self.xT[:, k, :ntok], start=(k == 0), stop=(k == 15)),
                              reads=[wB] + xTb, writes=[self.bank[bk]], signal=(k == 15))
                sgt, sgB = self.sg[par], self.sgB[par]
                sc.op("act", lambda e, bg=bg, sgt=sgt: e.activation(out=sgt[:, :ntok], in_=self.pbank(bg)[:, :ntok], func=AF.Silu),
                      [self.bank[bg]], [sgB])
                sc.op("dve", lambda e, bu=bu, sgt=sgt, f=f: e.tensor_tensor(out=self.hT[:, f, :ntok], in0=sgt[:, :ntok], in1=self.pbank(bu)[:, :ntok], op=ALU.mult),
                      [self.bank[bu], sgB], [self.hTB])
        cres = 0.5 / self.ALPHA
        for half in range(2):
            for fc4 in range(11):
                wB, wap = self.wblock("wd", (l * 2 + j) * DFF + fc4 * 512, 4, half * 1024, 1024, wdB)
                for fcl in range(4):
                    fc = fc4 * 4 + fcl
                    for s, tn in subs:
                        for jj in range(2):
                            bk = s * 2 + jj
                            sc.op("pe", lambda e, bk=bk, tn=tn, s=s, fc=fc, wap=wap, fcl=fcl, jj=jj: e.matmul(
                                self.pbank(bk)[:tn, :], lhsT=self.hT[:, fc, s * 128: s * 128 + tn], rhs=wap[:, fcl, jj * 512:(jj + 1) * 512],
                                start=(fc == 0), stop=(fc == 43)),
                                reads=[wB, self.hTB], writes=[self.bank[bk]],
                                signal=(fc == 43 or (fcl == 3 and s == subs[-1][0] and jj == 1)))
            for s, tn in subs:
                for jj in range(2):
                    bk = s * 2 + jj
                    xs = self.x[:tn, s, half * 1024 + jj * 512: half * 1024 + (jj + 1) * 512]
                    sc.op("dve", lambda e, bk=bk, tn=tn, xs=xs: e.scalar_tensor_tensor(out=xs, in0=self.pbank(bk)[:tn, :], scalar=cres, in1=xs,
                                                                                       op0=ALU.mult, op1=ALU.add),
                          [self.bank[bk], self.xB[s]], [self.xB[s]])
        self.post_norm(subs)


    def mm(self, out, lhsT, rhs, start, stop, reads, writes, signal):
        self.sc.op("pe", lambda e: e.matmul(out, lhsT=lhsT, rhs=rhs, start=start, stop=stop), reads, writes, signal)

    def tr(self, out, in_, n, reads, writes, signal):
        self.sc.op("pe", lambda e: e.transpose(out=out, in_=in_, identity=self.ident[:n, :n]), list(reads) + [self.constB], writes, signal)

    def actf(self, out, in_, func, reads, writes, bias=None, scale=None, accum=None):
        kw = {}
        if bias is not None:
            kw["bias"] = bias
        if scale is not None:
            kw["scale"] = scale
        if accum is not None:
            kw["accum_out"] = accum
        self.sc.op("act", lambda e: e.activation(out=out, in_=in_, func=func, **kw), reads, writes)

    def tt(self, eng, out, in0, in1, op, reads, writes):
        self.sc.op(eng, lambda e: e.tensor_tensor(out=out, in0=in0, in1=in1, op=op), reads, writes)

    def ts(self, eng, out, in0, s1, s2, op0, op1, reads, writes):
        if s2 is None:
            self.sc.op(eng, lambda e: e.tensor_scalar(out=out, in0=in0, scalar1=s1, scalar2=None, op0=op0), reads, writes)
        else:
            self.sc.op(eng, lambda e: e.tensor_scalar(out=out, in0=in0, scalar1=s1, scalar2=s2, op0=op0, op1=op1), reads, writes)

    def stt(self, eng, out, in0, scalar, in1, op0, op1, reads, writes):
        self.sc.op(eng, lambda e: e.scalar_tensor_tensor(out=out, in0=in0, scalar=scalar, in1=in1, op0=op0, op1=op1), reads, writes)

    def dmaq(self, q, out, in_, reads, writes, owner):
        self.sc.dma(q, lambda e: e.dma_start(out=out, in_=in_), reads, writes, owner)

    def tm_linear(self, wname, wB, r0, KC, c0, ncols, pw, src, srcB, subs, consume, b0=0, bias=None):
        nb = (pw + 511) // 512
        kcb = max(1, min(KC, 4096 // pw))
        for pc in range(0, ncols, pw):
            for kb in range(0, KC, kcb):
                nk = min(kcb, KC - kb)
                blkB, blk = self.wblock(wname, r0 + kb * 128, nk, c0 + pc, pw, wB)
                for kl in range(nk):
                    k = kb + kl
                    for si, (s, tn) in enumerate(subs):
                        for jj in range(nb):
                            w = min(512, pw - jj * 512)
                            bk = b0 + si * nb + jj
                            stop = (k == KC - 1) and bias is None
                            lastuse = (kl == nk - 1 and si == len(subs) - 1 and jj == nb - 1)
                            self.mm(self.pbank(bk)[:tn, :w], src[:, k, s * 128: s * 128 + tn], blk[:, kl, jj * 512: jj * 512 + w],
                                    k == 0, stop, [blkB] + list(srcB), [self.bank[bk]], stop or lastuse)
            if bias is not None:
                bap, bB = bias
                for si, (s, tn) in enumerate(subs):
                    for jj in range(nb):
                        w = min(512, pw - jj * 512)
                        bk = b0 + si * nb + jj
                        self.mm(self.pbank(bk)[:tn, :w], self.ones[0:1, :tn], bap[0:1, pc + jj * 512: pc + jj * 512 + w],
                                False, True, [bB, self.constB], [self.bank[bk]], True)
            for si, (s, tn) in enumerate(subs):
                for jj in range(nb):
                    w = min(512, pw - jj * 512)
                    consume(pc, s, tn, jj, w, b0 + si * nb + jj)

    def fm_linear(self, wname, wB, r0, c0, nchunks, src, srcB, ntok, consume, banks=(4, 5, 6, 7)):
        for bi in range((nchunks + 1) // 2):
            ncb = min(2, nchunks - bi * 2)
            blkB, blk = self.wblock(wname, r0, 16, c0 + bi * 256, ncb * 128, wB)
            for c2 in range(ncb):
                ci = bi * 2 + c2
                bk = banks[ci % len(banks)]
                for k in range(16):
                    self.mm(self.pbank(bk)[:, :ntok], blk[:, k, c2 * 128:(c2 + 1) * 128], src[:, k, :ntok], k == 0, k == 15,
                            [blkB] + list(srcB), [self.bank[bk]], k == 15)
                consume(ci, bk)

    def residual_consume(self, c):
        def f(pc, s, tn, jj, w, bk):
            xs = self.x[:tn, s, pc + jj * 512: pc + jj * 512 + w]
            self.stt("dve", xs, self.pbank(bk)[:tn, :w], c, xs, ALU.mult, ALU.add, [self.bank[bk], self.xB[s]], [self.xB[s]])
        return f

    def rope_fm(self, bk, kind, dst, dstBs, ntok):
        qi = self.cnt2 % 2
        self.cnt2 += 1
        qr, qrB = self.qraw[qi], self.qrawB[qi]
        rb = 2 + qi
        self.actf(qr[:, :ntok], self.pbank(bk)[:, :ntok], AF.Copy, [self.bank[bk]], [qrB])
        rot = self.rotA if kind == 0 else self.rotB
        self.mm(self.pbank(rb)[:, :ntok], rot[:, :], qr[:, :ntok], True, True, [qrB, self.constB], [self.bank[rb]], True)
        t1, t1B, t2, t2B = self.sg[0], self.sgB[0], self.sg[1], self.sgB[1]
        cos, sin = self.tabf[:, 2 * kind, :ntok], self.tabf[:, 2 * kind + 1, :ntok]
        self.tt("dve", t1[:, :ntok], self.pbank(bk)[:, :ntok], cos, ALU.mult, [self.bank[bk], self.tabfB], [t1B])
        self.tt("dve", t2[:, :ntok], self.pbank(rb)[:, :ntok], sin, ALU.mult, [self.bank[rb], self.tabfB], [t2B])
        self.tt("pool", dst, t1[:, :ntok], t2[:, :ntok], ALU.add, [t1B, t2B], dstBs)

    def rope_tm(self, buf, bufB, tn, s, kind, nh, hd):
        half = 16 if kind == 0 else 8
        o = 0 if kind == 0 else 32
        v = buf[:tn, 0:nh * hd].rearrange("p (h d) -> p h d", h=nh)
        x1, x2 = v[:, :, 0:half], v[:, :, half:2 * half]
        t = [self.rtmp[:tn, i, 0:nh * half].rearrange("p (h d) -> p h d", h=nh) for i in range(4)]
        R = [bufB, self.tabtB, self.rtmpB]
        for h in range(nh):
            cos = self.tabt[:tn, s, o:o + half]
            sin = self.tabt[:tn, s, o + half:o + 2 * half]
            self.tt("dve", t[0][:, h, :], x1[:, h, :], cos, ALU.mult, R, [self.rtmpB])
            self.tt("dve", t[1][:, h, :], x2[:, h, :], sin, ALU.mult, R, [self.rtmpB])
            self.tt("dve", t[2][:, h, :], x1[:, h, :], sin, ALU.mult, R, [self.rtmpB])
            self.tt("dve", t[3][:, h, :], x2[:, h, :], cos, ALU.mult, R, [self.rtmpB])
        self.tt("dve", x1, t[0], t[1], ALU.subtract, R, [bufB])
        self.tt("dve", x2, t[2], t[3], ALU.add, R, [bufB])

    def kT_from_tok(self, l, q, grp, tn, pos):
        self.actf(self.kb16[:tn, :], self.ktok[:tn, :], AF.Copy, [self.ktokB], [self.kb16B])
        pv = self.pbank16(7)
        for i in range(4):
            self.tr(pv[:, i * 128: i * 128 + tn], self.kb16[:tn, i * 128:(i + 1) * 128], tn, [self.kb16B], [self.bank[7]], i == 3)
        self.copy("dve", self.kTs[:, :, :tn], pv[:, 0:512].rearrange("p (h t) -> p h t", h=4)[:, :, :tn], [self.bank[7]], [self.kTsB])
        nm = f"kT{l}_{q}"
        dst = self.d[nm][grp * 4:(grp + 1) * 4, :, pos:pos + tn].rearrange("h p t -> p h t")
        self.dmaq("pool", dst, self.kTs[:, :, :tn], [self.kTsB], [self.dbuf[nm]], self.kTsB)

    def ki_tail(self, l, q, tn, pos):
        self.actf(self.kb16[:tn, 0:64], self.ktok[:tn, 0:64], AF.Copy, [self.ktokB], [self.kb16B])
        self.actf(self.kb16[:tn, 64:128], self.ktok[:tn, 0:64], AF.Copy, [self.ktokB], [self.kb16B])
        pv = self.pbank16(7)
        self.tr(pv[:, 0:tn], self.kb16[:tn, 0:128], tn, [self.kb16B], [self.bank[7]], True)
        self.copy("dve", self.kTs[:, 0, :tn], pv[:, 0:tn], [self.bank[7]], [self.kTsB])
        nm = f"iT{l}_{q}"
        self.dmaq("pool", self.d[nm][:, pos:pos + tn], self.kTs[:, 0, :tn], [self.kTsB], [self.dbuf[nm]], self.kTsB)

    def k_consume(self, l, q, pos0, row0, samp, kind, grp0):
        okn = f"ok{l}s" if samp else f"ok{l}p"
        def f(pc, s, tn, jj, w, bk):
            grp = grp0 + pc // 512
            self.actf(self.ktok[:tn, :], self.pbank(bk)[:tn, :], AF.Copy, [self.bank[bk]], [self.ktokB])
            self.rope_tm(self.ktok, self.ktokB, tn, s, kind, 4 if kind == 0 else 8, 128 if kind == 0 else 64)
            self.dmaq("pool", self.d[okn][row0 + s * 128: row0 + s * 128 + tn, grp * 512:(grp + 1) * 512], self.ktok[:tn, :],
                      [self.ktokB], [self.dbuf[okn]], self.ktokB)
            self.kT_from_tok(l, q, grp, tn, pos0 + s * 128)
        return f

    def v_consume(self, l, q, pos0, row0, samp, grp0):
        ovn = f"ov{l}s" if samp else f"ov{l}p"
        def f(pc, s, tn, jj, w, bk):
            grp = grp0 + pc // 512
            self.actf(self.vtok[:tn, :], self.pbank(bk)[:tn, :], AF.Copy, [self.bank[bk]], [self.vtokB])
            self.dmaq("pool", self.d[ovn][row0 + s * 128: row0 + s * 128 + tn, grp * 512:(grp + 1) * 512], self.vtok[:tn, :],
                      [self.vtokB], [self.dbuf[ovn]], self.vtokB)
            self.copy("dve", self.vb16[:tn, :], self.vtok[:tn, :], [self.vtokB], [self.vb16B])
            nm = f"vS{l}_{q}"
            kb = (pos0 + s * 128) // 128
            dst = self.d[nm][grp * 4:(grp + 1) * 4, 0:tn, kb, :].rearrange("h p d -> p h d")
            self.dmaq("pool", dst, self.vb16[:tn, :].rearrange("p (h d) -> p h d", h=4), [self.vb16B], [self.dbuf[nm]], self.vb16B)
        return f

    def ki_consume(self, l, q, pos0, row0, samp):
        oin = f"oi{l}s" if samp else f"oi{l}p"
        def f(pc, s, tn, jj, w, bk):
            self.actf(self.ktok[:tn, 0:80], self.pbank(bk)[:tn, 0:80], AF.Copy, [self.bank[bk]], [self.ktokB])
            self.rope_tm(self.ktok, self.ktokB, tn, s, 1, 1, 64)
            self.dmaq("pool", self.d[oin][row0 + s * 128: row0 + s * 128 + tn, :], self.ktok[:tn, 0:64], [self.ktokB], [self.dbuf[oin]], self.ktokB)
            self.ts("dve", self.wi[:tn, s, :], self.ktok[:tn, 64:80], 1.0 / 32.0, None, ALU.mult, None, [self.ktokB], [self.wiB])
            self.ki_tail(l, q, tn, pos0 + s * 128)
        return f

    def convert_cache(self, l):
        P, d = self, self.d
        q = P.NPS
        dsa = (l % 2 == 0)
        nh = 4 if dsa else 16
        KBP = P.PAST // 128
        cvB = Buf(f"cvt{l}")
        for h in range(nh):
            src = d[f"cv{l}"][0:P.PAST, h * 128:(h + 1) * 128].rearrange("(kb p) d -> p kb d", p=128)
            self.vcast.append(lambda h=h, src=src, l=l: self.dmaq("pool", d[f"vS{l}_{q}"][h, :, 0:KBP, :], src, [], [self.dbuf[f"vS{l}_{q}"]], cvB))
        for kb in range(KBP):
            for grp in range(nh // 4):
                self.dmaq("sp", self.ktok[:, :], d[f"ck{l}"][kb * 128:(kb + 1) * 128, grp * 512:(grp + 1) * 512], [], [self.ktokB], self.ktokB)
                self.kT_from_tok(l, q, grp, 128, kb * 128)
            if dsa:
                self.dmaq("sp", self.ktok[:, 0:64], d[f"ci{l}"][kb * 128:(kb + 1) * 128, :], [], [self.ktokB], self.ktokB)
                self.ki_tail(l, q, 128, kb * 128)

    def stiles(self, L):
        return [(c0, min(512, L - c0)) for c0 in range(0, L, 512)]

    def kblocks(self, L):
        return [(kb, min(128, L - kb * 128)) for kb in range((L + 127) // 128)]

    def load_kv(self, l, q, h, L, slot):
        KB = (L + 127) // 128
        self.dmaq("sp", self.kvk[slot][:, 0:L], self.d[f"kT{l}_{q}"][h, :, 0:L], [self.dbuf[f"kT{l}_{q}"]], [self.kvB[slot]], self.kvB[slot])
        KF, rem = L // 128, L % 128
        self.dmaq("sp", self.kvv[slot][:, 0:KF, 0:128], self.d[f"vS{l}_{q}"][h, :, 0:KF, :], [self.dbuf[f"vS{l}_{q}"]], [self.kvB[slot]], self.kvB[slot])
        if rem:
            self.dmaq("sp", self.kvv[slot][0:rem, KF, 0:128], self.d[f"vS{l}_{q}"][h, 0:rem, KF, :], [self.dbuf[f"vS{l}_{q}"]], [self.kvB[slot]], self.kvB[slot])

    def pv_from_P(self, tq, L, slot, obank_ap):
        kbs = self.kblocks(L)
        groups, cur = [], []
        for kb, nk in kbs:
            if nk < 128:
                if cur:
                    groups.append(cur)
                groups.append([(kb, nk)])
                cur = []
            else:
                cur.append((kb, nk))
                if len(cur) == 8:
                    groups.append(cur)
                    cur = []
        if cur:
            groups.append(cur)
        for gi, grp in enumerate(groups):
            b = 5 + (self.cnt3 % 2)
            self.cnt3 += 1
            pv = self.pbank16(b)
            nk = grp[0][1]
            for i, (kb, _) in enumerate(grp):
                self.tr(pv[:nk, i * 128: i * 128 + tq], self.Pm[:tq, kb * 128: kb * 128 + nk], tq, [self.PmB], [self.bank[b]], i == len(grp) - 1)
            n = len(grp)
            srcv = pv[:nk, 0:n * 128].rearrange("p (c t) -> p c t", c=n)[:, :, :tq]
            self.copy(self.alt(), self.PT[:nk, grp[0][0]:grp[0][0] + n, :tq], srcv, [self.bank[b]], [self.PTB])
        if obank_ap is not None:
            self.pv_only(tq, L, slot, obank_ap)

    def pv_only(self, tq, L, slot, obank_ap):
        kbs = self.kblocks(L)
        for kb, nk in kbs:
            self.mm(obank_ap, self.PT[:nk, kb, :tq], self.kvv[slot][:nk, kb, 0:129], kb == 0, kb == len(kbs) - 1,
                    [self.PTB, self.kvB[slot]], [self.bank[7]], kb == len(kbs) - 1)

    def make_oT(self, s, tq):
        for half in range(2):
            b = 5 + half
            pv = self.pbank16(b)
            for c in range(8):
                cc = half * 8 + c
                self.tr(pv[:, c * 128: c * 128 + tq], self.otok[:tq, cc * 128:(cc + 1) * 128], tq, [self.otokB], [self.bank[b]], c == 7)
            srcv = pv[:, 0:1024].rearrange("p (c t) -> p c t", c=8)[:, :, 0:tq]
            self.copy(self.alt(), self.qoT[:, half * 8:(half + 1) * 8, s * 128: s * 128 + tq], srcv, [self.bank[b]], [self.qoTB[s]])

    def attend_dsa(self, l, q, s, tq, L, masked):
        P = self
        TOPK = min(256, ((P.PAST + P.DEC) if not masked else P.S) // 4)
        assert TOPK % 8 == 0
        scale = 128 ** -0.5
        st = self.stiles(L)
        qc = slice(s * 128, s * 128 + tq)
        sbanks = [self.bank[i] for i in range(len(st))]
        self.dmaq("sp", self.kiT[:, 0:L], self.d[f"iT{l}_{q}"][:, 0:L], [self.dbuf[f"iT{l}_{q}"]], [self.kiTB], self.kiTB)
        for (c0, w) in st:
            for h in range(16):
                pb = 64 * (h % 2)
                i2 = self.cnt4 % 2
                self.cnt4 += 1
                bk = 5 + i2
                tmp, tmpB = self.sg[i2], self.sgB[i2]
                self.mm(self.pbank(bk)[:tq, :w], self.qiT[pb:pb + 64, h // 2, qc], self.kiT[pb:pb + 64, c0:c0 + w], True, True,
                        [self.qiTB[s], self.kiTB], [self.bank[bk]], True)
                self.actf(tmp[:tq, :w], self.pbank(bk)[:tq, :w], AF.Relu, [self.bank[bk]], [tmpB])
                if h == 0:
                    self.ts("dve", self.A[:tq, c0:c0 + w], tmp[:tq, :w], self.wi[:tq, s, 0:1], None, ALU.mult, None, [tmpB, self.wiB], [self.AB])
                else:
                    self.stt("dve", self.A[:tq, c0:c0 + w], tmp[:tq, :w], self.wi[:tq, s, h:h + 1], self.A[:tq, c0:c0 + w], ALU.mult, ALU.add,
                             [tmpB, self.wiB, self.AB], [self.AB])
        if masked:
            self.sc.op("dve", lambda e: e.memset(self.A[0:64, L - 64:L], -BIG), [self.AB], [self.AB])
        if L > TOPK:
            self.copy("act", self.Bf[:tq, 0:L], self.A[:tq, 0:L], [self.AB], [self.BfB])
            for r in range(TOPK // 8):
                self.sc.op("dve", lambda e: e.max(out=self.m8[:tq, :], in_=self.Bf[:tq, 0:L]), [self.BfB], [self.smB])
                self.sc.op("dve", lambda e: e.match_replace(out=self.Bf[:tq, 0:L], in_to_replace=self.m8[:tq, :], in_values=self.Bf[:tq, 0:L],
                                                             imm_value=-BIG), [self.smB, self.BfB], [self.BfB])
            self.sc.op("dve", lambda e: e.tensor_reduce(out=self.sm[:tq, 0:1], in_=self.m8[:tq, :], axis=AX.X, op=ALU.min), [self.smB], [self.smB])
            self.ts("dve", self.Bb[:tq, 0:L], self.A[:tq, 0:L], self.sm[:tq, 0:1], -BIG, ALU.is_lt, ALU.mult, [self.AB, self.smB], [self.BfB])
        else:
            self.ts("dve", self.Bb[:tq, 0:L], self.A[:tq, 0:L], -BIG / 2, -BIG, ALU.is_lt, ALU.mult, [self.AB], [self.BfB])
        sflat = self.ps[:, 0:5, :].rearrange("p b n -> p (b n)")
        sB = [self.bank[i] for i in range(len(st))]
        slots = {}

        def qk(h):
            g = h // 4
            if h % 4 == 0:
                slots[g] = self.cnt5 % 2
                self.cnt5 += 1
                self.load_kv(l, q, g, L, slots[g])
            slot = slots[g]
            for i, (c0, w) in enumerate(st):
                self.mm(self.pbank(i)[:tq, :w], self.qoT[:, h, qc], self.kvk[slot][:, c0:c0 + w], True, False,
                        [self.qoTB[s], self.kvB[slot]], [self.bank[i]], False)
                self.mm(self.pbank(i)[:tq, :w], self.ident[:tq, :tq], self.Bb[:tq, c0:c0 + w], False, True,
                        [self.BfB, self.constB, self.qoTB[s], self.kvB[slot]], [self.bank[i]], True)
                self.sc.op("dve", lambda e, i=i, w=w: e.reduce_max(out=self.nmx4[:tq, i:i + 1], in_=self.pbank(i)[:tq, :w], axis=AX.X),
                           [self.bank[i]], [self.nmx4B])

        def fin():
            self.sc.op("dve", lambda e: e.tensor_reduce(out=self.nmx[:tq, 0:1], in_=self.nmx4[:tq, 0:len(st)], axis=AX.X, op=ALU.max, negate=True),
                       [self.nmx4B], [self.nmxB])
            self.actf(self.Pm[:tq, 0:L], sflat[:tq, 0:L], AF.Exp, sB + [self.nmxB], [self.PmB], bias=self.nmx[:tq, 0:1], scale=1.0)

        qk(0)
        fin()
        for h in range(16):
            slot = slots[h // 4]
            self.pv_from_P(tq, L, slot, None)
            if h + 1 < 16:
                qk(h + 1)
                fin()
            self.pv_only(tq, L, slot, self.pbank(7)[:tq, 0:129])
            self.sc.op("dve", lambda e: e.reciprocal(out=self.sm[:tq, 3:4], in_=self.pbank(7)[:tq, 128:129]), [self.bank[7]], [self.smB])
            self.actf(self.otok[:tq, h * 128:(h + 1) * 128], self.pbank(7)[:tq, 0:128], AF.Copy, [self.bank[7], self.smB], [self.otokB],
                      scale=self.sm[:tq, 3:4])
        self.make_oT(s, tq)

    def attend_diff(self, l, q, s, tq, L, masked):
        j = l // 2
        scale = 64 ** -0.5
        st = self.stiles(L)
        qc = slice(s * 128, s * 128 + tq)
        sflat = self.ps[:, 0:5, :].rearrange("p b n -> p (b n)")
        sB = [self.bank[i] for i in range(len(st))]
        slots = {}

        def qk(h, c):
            if c == 0:
                slots[h] = self.cnt5 % 2
                self.cnt5 += 1
                self.load_kv(l, q, h, L, slots[h])
            slot = slots[h]
            pb = 64 * c
            R_ = [self.qoTB[s], self.kvB[slot]]
            for i, (c0, w) in enumerate(st):
                last = masked and (i == len(st) - 1)
                w0 = w - 64 if last else w
                if w0 > 0:
                    self.mm(self.pbank(i)[:tq, :w0], self.qoT[pb:pb + 64, h, qc], self.kvk[slot][pb:pb + 64, c0:c0 + w0], True, True,
                            R_, [self.bank[i]], not last)
                if last:
                    self.mm(self.pbank(i)[:tq, w0:w], self.qoT[pb:pb + 64, h, qc], self.kvk[slot][pb:pb + 64, c0 + w0:c0 + w], True, False,
                            R_, [self.bank[i]], False)
                    self.mm(self.pbank(i)[:tq, w0:w], self.ident[:tq, :tq], self.cmaskt[:tq, :], False, True,
                            R_ + [self.constB], [self.bank[i]], True)
                self.sc.op("dve", lambda e, i=i, w=w: e.reduce_max(out=self.nmx4[:tq, i:i + 1], in_=self.pbank(i)[:tq, :w], axis=AX.X),
                           [self.bank[i]], [self.nmx4B])

        def fin():
            self.sc.op("dve", lambda e: e.tensor_reduce(out=self.nmx[:tq, 0:1], in_=self.nmx4[:tq, 0:len(st)], axis=AX.X, op=ALU.max, negate=True),
                       [self.nmx4B], [self.nmxB])
            self.actf(self.Pm[:tq, 0:L], sflat[:tq, 0:L], AF.Exp, sB + [self.nmxB], [self.PmB], bias=self.nmx[:tq, 0:1], scale=1.0)

        items = [(h, c) for h in range(16) for c in range(2)]
        qk(0, 0)
        fin()
        for ii, (h, c) in enumerate(items):
            slot = slots[h]
            self.pv_from_P(tq, L, slot, None)
            if ii + 1 < len(items):
                qk(*items[ii + 1])
                fin()
            self.pv_only(tq, L, slot, self.pbank(7)[:tq, c * 129:(c + 1) * 129])
            if c == 0:
                continue
            rsv = self.pbank(7)[:tq, 0:258].rearrange("p (c d) -> p c d", c=2)[:, :, 128]
            self.sc.op("dve", lambda e: e.reciprocal(out=self.sm[:tq, 4:6], in_=rsv), [self.bank[7]], [self.smB])
            self.ts("dve", self.sm[:tq, 6:7], self.sm[:tq, 5:6], self.lamt[:tq, 4 * j:4 * j + 1], None, ALU.mult, None, [self.smB, self.constB], [self.smB])
            self.ts("dve", self.osm[:tq, 0, :], self.pbank(7)[:tq, 129:257], self.sm[:tq, 6:7], None, ALU.mult, None, [self.bank[7], self.smB], [self.osmB])
            self.stt("dve", self.osm[:tq, 1, :], self.pbank(7)[:tq, 0:128], self.sm[:tq, 4:5], self.osm[:tq, 0, :], ALU.mult, ALU.subtract,
                     [self.bank[7], self.smB, self.osmB], [self.osmB])
            self.sc.op("dve", lambda e: e.scalar_tensor_tensor(out=self.osm[:tq, 0, :], in0=self.osm[:tq, 1, :], scalar=1.0, in1=self.osm[:tq, 1, :],
                                                               op0=ALU.mult, op1=ALU.mult, accum_out=self.sm[:tq, 7:8]), [self.osmB], [self.osmB, self.smB])
            self.ts("dve", self.sm[:tq, 8:9], self.sm[:tq, 7:8], 1.0 / 128.0, LN_EPS, ALU.mult, ALU.add, [self.smB], [self.smB])
            self.actf(self.sm[:tq, 9:10], self.sm[:tq, 8:9], AF.Ln, [self.smB], [self.smB])
            self.actf(self.sm[:tq, 10:11], self.sm[:tq, 9:10], AF.Exp, [self.smB], [self.smB], scale=-0.5)
            self.stt("dve", self.otok[:tq, h * 128:(h + 1) * 128], self.osm[:tq, 1, :], self.sm[:tq, 10:11], self.gs[:tq, j, :], ALU.mult, ALU.mult,
                     [self.osmB, self.smB, self.constB], [self.otokB])
        self.make_oT(s, tq)

    def tile_tables(self, tix0, subs, ntok):
        d = self.d
        self.tix0 = tix0
        for s, tn in subs:
            self.dmaq("sp", self.tabt[:tn, s, :], d["tabtm"][tix0 + s * 128: tix0 + s * 128 + tn, :], [], [self.tabtB], self.tabtB)

    def mixer(self, l, q, pos0, row0, subs, ntok, samp):
        P = self
        j = l // 2
        dsa = (l % 2 == 0)
        self.sc.fence(self.attB)
        self.load_gb(l * 4 + 1)
        for t, src in enumerate((0, 1, 2, 3) if dsa else (0, 1, 4, 5)):
            self.dmaq("sp", self.tabf[:, t, :ntok], self.d["tabfm"][src, :, self.tix0:self.tix0 + ntok], [], [self.tabfB], self.tabfB)
        for i in range(2):
            self.sc.op("dve", lambda e, i=i: e.memset(self.kvv[i][:, :, 128:129], 1.0), [self.kvB[i]], [self.kvB[i]])
        while P.DEC and not samp and self.vcast and self.vcast_quota > 0:
            self.vcast.pop(0)()
            self.vcast_quota -= 1
        self.vcast_quota = 2
        xTb = [self.xTB[s] for s, _ in subs]
        qoB = [self.qoTB[s] for s, _ in subs]
        qiB = [self.qiTB[s] for s, _ in subs]
        dbg = self.cfg["MIXDBG"]
        if dbg < 1:
            return
        if dsa:
            wi_, wo_ = self.wsl[("awi", j * D)], self.wsl[("awo", j * D)]
            self.fm_linear("awi", wi_, j * D, 0, 16, self.xT, xTb, ntok,
                           lambda ci, bk: self.rope_fm(bk, 0, self.qoT[:, ci, :ntok], qoB, ntok))
            self.fm_linear("awi", wi_, j * D, 3072, 8, self.xT, xTb, ntok,
                           lambda ci, bk: self.rope_fm(bk, 1, self.qiT[:, ci, :ntok], qiB, ntok))
            if dbg < 2:
                return
            self.tm_linear("awi", wi_, j * D, 16, 2048, 512, 512, self.xT, xTb, subs, self.k_consume(l, q, pos0, row0, samp, 0, 0))
            self.tm_linear("awi", wi_, j * D, 16, 2560, 512, 512, self.xT, xTb, subs, self.v_consume(l, q, pos0, row0, samp, 0))
            self.tm_linear("awi", wi_, j * D, 16, 4096, 80, 80, self.xT, xTb, subs, self.ki_consume(l, q, pos0, row0, samp))
            if dbg < 3:
                return
            for s, tn in subs:
                L = (P.PAST + P.DEC) if samp else pos0 + (s + 1) * 128
                self.attend_dsa(l, q, s, tn, L, not samp)
            if dbg < 4:
                return
            self.tm_linear("awo", wo_, j * D, 16, 0, 2048, 1024, self.qoT, qoB, subs, self.residual_consume(1.0 / self.ALPHA))
        else:
            wi_, wo_ = self.wsl[("bwi", j * D)], self.wsl[("bwo", j * D)]
            self.fm_linear("bwi", wi_, j * D, 0, 16, self.xT, xTb, ntok,
                           lambda ci, bk: self.rope_fm(bk, 1, self.qoT[:, ci, :ntok], qoB, ntok))
            self.tm_linear("bwi", wi_, j * D, 16, 2048, 2048, 512, self.xT, xTb, subs, self.k_consume(l, q, pos0, row0, samp, 1, 0))
            self.tm_linear("bwi", wi_, j * D, 16, 4096, 2048, 512, self.xT, xTb, subs, self.v_consume(l, q, pos0, row0, samp, 0))
            for s, tn in subs:
                L = (P.PAST + P.DEC) if samp else pos0 + (s + 1) * 128
                self.attend_diff(l, q, s, tn, L, not samp)
            self.tm_linear("bwo", wo_, j * D, 16, 0, 2048, 1024, self.qoT, qoB, subs, self.residual_consume(1.0 / self.ALPHA))
        self.post_norm(subs)

    def ple(self, l, row0, subs, ntok, samp):
        d = self.d
        self.load_gb(l * 4 + 3)
        pin = d["ps"] if samp else d["pp"]
        wgB_, wpB_, wbB_ = self.wsl[("plg", l * D)], self.wsl[("plp", l * PLE)], self.wsl[("plb", 0)]
        self.dmaq("sp", self.brow[0:1, :], d["plb_b"][l:l + 1, :], [wbB_], [self.browB], self.browB)
        for s, tn in subs:
            self.dmaq("sp", self.ptok[:tn, :], pin[l, row0 + s * 128: row0 + s * 128 + tn, :], [], [self.ptokB], self.ptokB)
            self.copy("act", self.pb16[:tn, :], self.ptok[:tn, :], [self.ptokB], [self.pb16B])
            pv = self.pbank16(7)
            for c in range(2):
                self.tr(pv[:, c * 128: c * 128 + tn], self.pb16[:tn, c * 128:(c + 1) * 128], tn, [self.pb16B], [self.bank[7]], c == 1)
            self.copy("dve", self.pT[:, :, s * 128: s * 128 + tn], pv[:, 0:256].rearrange("p (c t) -> p c t", c=2)[:, :, :tn], [self.bank[7]], [self.pTB])
        xTb = [self.xTB[s] for s, _ in subs]
        inv_a = 1.0 / self.ALPHA
        for p0 in range(0, len(subs), 2):
            pair = subs[p0:p0 + 2]
            for half in range(2):
                wpB, wp = self.wblock("plp", l * PLE, 2, 0, 2048, wpB_)
                for si, (s, tn) in enumerate(pair):
                    for jj in range(2):
                        bk = 4 + si * 2 + jj
                        cs = half * 1024 + jj * 512
                        for c in range(2):
                            self.mm(self.pbank(bk)[:tn, :], self.pT[:, c, s * 128: s * 128 + tn], wp[:, c, cs:cs + 512], c == 0, c == 1,
                                    [self.pTB, wpB], [self.bank[bk]], c == 1)

                def cons(pc, s, tn, jj, w, bk, half=half, pair=pair):
                    si = [x[0] for x in pair].index(s)
                    pbk = 4 + si * 2 + jj
                    i2 = self.cnt4 % 2
                    self.cnt4 += 1
                    tmp, tmpB = self.sg[i2], self.sgB[i2]
                    self.actf(tmp[:tn, :], self.pbank(bk)[:tn, :], AF.Sigmoid, [self.bank[bk]], [tmpB])
                    self.tt("dve", tmp[:tn, :], tmp[:tn, :], self.pbank(pbk)[:tn, :], ALU.mult, [tmpB, self.bank[pbk]], [tmpB])
                    xs = self.x[:tn, s, pc + jj * 512: pc + (jj + 1) * 512]
                    self.stt("dve", xs, tmp[:tn, :], inv_a, xs, ALU.mult, ALU.add, [tmpB, self.xB[s]], [self.xB[s]])
                self.tm_linear("plg", wgB_, l * D, 16, half * 1024, 1024, 1024, self.xT, xTb, pair,
                               lambda pc, s, tn, jj, w, bk, half=half, cons=cons: cons(pc + half * 1024, s, tn, jj, w, bk),
                               b0=0, bias=(self.brow[0:1, half * 1024:(half + 1) * 1024], self.browB))
        self.post_norm(subs)


def rope_tables(P):
    pos = np.concatenate([np.arange(P.S), P.PAST + np.arange(P.DEC)]).astype(np.float32)
    out = {}
    for nm, half in (("A", 16), ("B", 8)):
        inv = (np.float32(THETA) ** (-(np.arange(half, dtype=np.float32) / np.float32(half)))).astype(np.float32)
        ang = (pos[:, None] * inv[None, :]).astype(np.float32)
        out[nm] = (np.cos(ang).astype(np.float32), np.sin(ang).astype(np.float32))
    tabtm = np.concatenate([out["A"][0], out["A"][1], out["B"][0], out["B"][1]], axis=1).astype(np.float32)
    tabfm = np.zeros((6, 128, P.NPOS), np.float32)
    tabfm[0] = 1.0
    tabfm[2] = 1.0
    cA, sA = out["A"]
    tabfm[0, 0:16] = cA.T; tabfm[0, 16:32] = cA.T
    tabfm[1, 0:16] = sA.T; tabfm[1, 16:32] = sA.T
    cB, sB = out["B"]
    for base in (0, 64):
        tabfm[2, base:base + 8] = cB.T; tabfm[2, base + 8:base + 16] = cB.T
        tabfm[3, base:base + 8] = sB.T; tabfm[3, base + 8:base + 16] = sB.T
    tabfm[4] = tabfm[2] * np.float32(0.125)
    tabfm[5] = tabfm[3] * np.float32(0.125)
    tabfm[0] = tabfm[0] * np.float32(128 ** -0.5)
    tabfm[1] = tabfm[1] * np.float32(128 ** -0.5)
    rotA = np.zeros((128, 128), np.float32)
    for m in range(16):
        rotA[m + 16, m] = -1.0
        rotA[m, m + 16] = 1.0
    rotB = np.zeros((128, 128), np.float32)
    for base in (0, 64):
        for m in range(8):
            rotB[base + m + 8, base + m] = -1.0
            rotB[base + m, base + m + 8] = 1.0
    return tabfm, tabtm, rotA, rotB


_PROG_CACHE = {}


def get_prog(cfg):
    key = tuple(sorted((k, str(v)) for k, v in cfg.items()))
    if key not in _PROG_CACHE:
        P = Prog(cfg)
        P.nc_built = P.build()
        _PROG_CACHE[key] = P
    return _PROG_CACHE[key]


def make_in_maps(P, inp, ncores):
    f = lambda a: np.ascontiguousarray(np.asarray(a, dtype=np.float32))
    DEP = P.DEPTH
    tabfm, tabtm, rotA, rotB = rope_tables(P)
    shared = {
        "ln_g": f(inp["ln_g"]).reshape(DEP * 4, D), "ln_b": f(inp["ln_b"]).reshape(DEP * 4, D),
        "wg": f(inp["ffn_w_gate"]).reshape(DEP * 2 * D, DFF), "wu": f(inp["ffn_w_up"]).reshape(DEP * 2 * D, DFF),
        "wd": f(inp["ffn_w_down"]).reshape(DEP * 2 * DFF, D),
        "plg": f(inp["ple_w_gate"]).reshape(DEP * D, D), "plb": f(inp["ple_b_gate"]).reshape(DEP, D),
        "plp": f(inp["ple_w_proj"]).reshape(DEP * PLE, D),
        "awi": f(inp["a_w_in"]).reshape(P.NA * D, A_IN), "awo": f(inp["a_w_out"]).reshape(P.NA * D, D),
        "ident": np.eye(128, dtype=np.float32), "rotA": rotA, "rotB": rotB, "tabfm": tabfm, "tabtm": tabtm,
    }
    if P.NB:
        shared["bwi"] = f(inp["b_w_in"]).reshape(P.NB * D, B_IN)
        shared["bwo"] = f(inp["b_w_out"]).reshape(P.NB * D, D)
        shared["blam"] = np.ascontiguousarray(np.concatenate([f(inp["b_lambda_q1"]), f(inp["b_lambda_k1"]),
                                                              f(inp["b_lambda_q2"]), f(inp["b_lambda_k2"])], axis=1))
        shared["bsub"] = f(inp["b_subln"])
    xp, pp = f(inp["x_prompt"]), f(inp["p_prompt"])
    xs, ps = f(inp["x_sample"]), f(inp["p_sample"])
    maps = []
    for c in range(ncores):
        m = dict(shared)
        sl = slice(c * P.NPS, (c + 1) * P.NPS)
        m["xp"] = np.ascontiguousarray(xp[sl].reshape(P.NPS * P.S, D))
        m["pp"] = np.ascontiguousarray(pp[:, sl].reshape(DEP, P.NPS * P.S, PLE))
        m["xs"] = np.ascontiguousarray(xs[c])
        m["ps"] = np.ascontiguousarray(ps[:, c])
        for l in range(DEP):
            m[f"ck{l}"] = np.ascontiguousarray(f(inp[f"cache_l{l}_k"])[c].reshape(P.PAST, -1))
            m[f"cv{l}"] = np.ascontiguousarray(f(inp[f"cache_l{l}_v"])[c].reshape(P.PAST, -1))
            if l % 2 == 0:
                m[f"ci{l}"] = np.ascontiguousarray(f(inp[f"cache_l{l}_kidx"])[c])
        maps.append(m)
    return maps


def assemble(P, res, ncores):
    def cat(name, shape_tail, per):
        return np.stack([r[name] for r in res], 0).reshape((ncores * per,) + shape_tail)
    outs = []
    yp = np.stack([r["yp"].reshape(P.NPS, P.S, D) for r in res], 0).reshape(ncores * P.NPS, P.S, D)
    ys = np.stack([r["ys"] for r in res], 0)
    outs += [yp, ys]
    for l in range(P.DEPTH):
        if l % 2 == 0:
            kshape, vshape = (4, 128), (4, 128)
        else:
            kshape, vshape = (16, 2, 64), (16, 128)
        kp = np.stack([r[f"ok{l}p"] for r in res], 0).reshape((ncores * P.NPS, P.S) + kshape)
        vp = np.stack([r[f"ov{l}p"] for r in res], 0).reshape((ncores * P.NPS, P.S) + vshape)
        ks = np.stack([r[f"ok{l}s"] for r in res], 0).reshape((ncores, P.DEC) + kshape)
        vs = np.stack([r[f"ov{l}s"] for r in res], 0).reshape((ncores, P.DEC) + vshape)
        if l % 2 == 0:
            ip = np.stack([r[f"oi{l}p"] for r in res], 0).reshape(ncores * P.NPS, P.S, 64)
            isx = np.stack([r[f"oi{l}s"] for r in res], 0).reshape(ncores, P.DEC, 64)
            outs += [kp, vp, ip, ks, vs, isx]
        else:
            outs += [kp, vp, ks, vs]
    return tuple(np.ascontiguousarray(o, dtype=np.float32) for o in outs)


def kernel(**inputs):
    cfg = default_cfg()
    P = get_prog(cfg)
    maps = make_in_maps(P, inputs, 8)
    r = run_bass_kernel_spmd(P.nc_built, maps, core_ids=list(range(8)))
    return assemble(P, r.results, 8)
```
